# Optimizing a Trainium2 kernel written in Bass

```python
import math
import jax, jax.numpy as jnp
from jax import lax
import numpy as np

D_MODEL = 2048
BATCH = 2
SEQ = 4096
DEPTH = 1

ATT_HEADS = 8
ATT_QK_DIM = 128
ATT_V_DIM = 2 * ATT_QK_DIM
ATT_WIDTH = ATT_HEADS * ATT_V_DIM
ATT_Q_WIDTH = ATT_HEADS * 2 * ATT_QK_DIM
Q_BLOCK = 128
SSM_HEADS = 32
SSM_HEAD_DIM = 64
SSM_WIDTH = SSM_HEADS * SSM_HEAD_DIM
SSM_GROUPS = 4
SSM_STATE = 128
SSM_CONV = 4
SSM_CHUNK = 128
SSM_CONV_DIM = SSM_WIDTH + 2 * SSM_GROUPS * SSM_STATE
MIX_WIDTH = ATT_WIDTH + SSM_WIDTH
IN_WIDTH = 2 * ATT_Q_WIDTH + ATT_WIDTH + SSM_WIDTH + SSM_CONV_DIM + SSM_HEADS
D_FF = -(-8 * D_MODEL // (3 * 256)) * 256
RMS_EPS = 1e-6
SUB_EPS = 1e-5

kernel_name = "hymba_diffattn_ssd_hybrid_block"


def rmsnorm(x, w, eps=RMS_EPS):
    xf = x.astype(jnp.float32)
    y = xf * lax.rsqrt(jnp.mean(xf * xf, axis=-1, keepdims=True) + eps)
    return (y * w.astype(jnp.float32)).astype(x.dtype)


def alibi_slopes(n):
    start = 2.0 ** (-8.0 / n)
    return jnp.asarray([start ** (i + 1) for i in range(n)], dtype=jnp.float32)


def diff_attention(q, k, v, lam, subln_w, lambda_init):
    b, s = q.shape[0], q.shape[1]
    nb = s // Q_BLOCK
    scale = ATT_QK_DIM ** -0.5
    slopes = alibi_slopes(ATT_HEADS)
    kpos = jnp.arange(s)
    qb = q.reshape(b, nb, Q_BLOCK, ATT_HEADS, 2, ATT_QK_DIM).transpose(1, 0, 2, 3, 4, 5)

    def one_block(args):
        q_blk, start = args
        qpos = start + jnp.arange(Q_BLOCK)
        dist = qpos[:, None] - kpos[None, :]
        bias = -slopes[:, None, None] * dist.astype(jnp.float32)
        sc = jnp.einsum('bqhcd,bkhcd->bhcqk', q_blk, k).astype(jnp.float32) * scale
        sc = sc + bias[None, :, None]
        sc = jnp.where(dist >= 0, sc, -jnp.inf)
        p = jax.nn.softmax(sc, axis=-1)
        p = p[:, :, 0] - lam * p[:, :, 1]
        return jnp.einsum('bhqk,bkhe->bqhe', p.astype(v.dtype), v)

    o = lax.map(one_block, (qb, jnp.arange(nb) * Q_BLOCK))
    o = o.transpose(1, 0, 2, 3, 4).reshape(b, s, ATT_HEADS, ATT_V_DIM)
    o = rmsnorm(o, subln_w, SUB_EPS) * (1.0 - lambda_init)
    return o.reshape(b, s, ATT_WIDTH)


def causal_dwconv(u, w, bias):
    c = u.shape[-1]
    out = lax.conv_general_dilated(u, w[:, None, :], window_strides=(1,), padding=[(SSM_CONV - 1, 0)],
                                   dimension_numbers=('NWC', 'WIO', 'NWC'), feature_group_count=c)
    return out + bias


def ssd_scan(xdt, adt, bm, cm):
    b, s, h, p = xdt.shape
    c = s // SSM_CHUNK
    g, r, n, l = SSM_GROUPS, h // SSM_GROUPS, SSM_STATE, SSM_CHUNK
    X = xdt.reshape(b, c, l, g, r, p)
    A = adt.reshape(b, c, l, g, r).transpose(0, 3, 4, 1, 2)
    Bm = bm.reshape(b, c, l, g, n)
    Cm = cm.reshape(b, c, l, g, n)
    A_cs = jnp.cumsum(A, axis=-1)
    seg = A_cs[..., :, None] - A_cs[..., None, :]
    causal = jnp.tril(jnp.ones((l, l), dtype=bool))
    Lmat = jnp.exp(jnp.where(causal, seg, -jnp.inf))
    CB = jnp.einsum('bclgn,bcsgn->bcgls', Cm, Bm)
    y_diag = jnp.einsum('bcgls,bgrcls,bcsgrp->bclgrp', CB, Lmat, X)
    decay_states = jnp.exp(A_cs[..., -1:] - A_cs)
    states = jnp.einsum('bcsgn,bgrcs,bcsgrp->bcgrpn', Bm, decay_states, X)
    chunk_decay = jnp.exp(A_cs[..., -1])

    def step(hstate, inp):
        st, dec = inp
        new = hstate * dec[..., None, None] + st
        return new, hstate

    h0 = jnp.zeros((b, g, r, p, n), dtype=jnp.float32)
    _, prev = lax.scan(step, h0, (states.transpose(1, 0, 2, 3, 4, 5), chunk_decay.transpose(3, 0, 1, 2)))
    prev = prev.transpose(1, 0, 2, 3, 4, 5)
    y_off = jnp.einsum('bclgn,bcgrpn,bgrcl->bclgrp', Cm, prev, jnp.exp(A_cs))
    return (y_diag + y_off).reshape(b, s, h, p)


def ssd_mixer(z, xbc, dt_raw, conv_w, conv_b, dt_bias, a_log, d_skip, norm_w):
    b, s = z.shape[0], z.shape[1]
    xbc = jax.nn.silu(causal_dwconv(xbc, conv_w, conv_b))
    gn = SSM_GROUPS * SSM_STATE
    xs, bm, cm = jnp.split(xbc, [SSM_WIDTH, SSM_WIDTH + gn], axis=-1)
    xs = xs.reshape(b, s, SSM_HEADS, SSM_HEAD_DIM)
    dt = jax.nn.softplus(dt_raw.astype(jnp.float32) + dt_bias.astype(jnp.float32))
    A = -jnp.exp(a_log.astype(jnp.float32))
    xf = xs.astype(jnp.float32)
    y = ssd_scan(xf * dt[..., None], dt * A,
                 bm.reshape(b, s, SSM_GROUPS, SSM_STATE).astype(jnp.float32),
                 cm.reshape(b, s, SSM_GROUPS, SSM_STATE).astype(jnp.float32))
    y = y + d_skip.astype(jnp.float32)[:, None] * xf
    y = y.reshape(b, s, SSM_WIDTH) * jax.nn.silu(z.astype(jnp.float32))
    yg = y.reshape(b, s, SSM_GROUPS, SSM_WIDTH // SSM_GROUPS)
    yg = yg * lax.rsqrt(jnp.mean(yg * yg, axis=-1, keepdims=True) + SUB_EPS)
    y = yg.reshape(b, s, SSM_WIDTH) * norm_w.astype(jnp.float32)
    return y.astype(z.dtype)


def setup_inputs(seed: int = 0) -> dict:
    key = jax.random.key(seed)
    ks = jax.random.split(key, 20)
    f32 = jnp.float32
    nrm = lambda k, shape, sc: jax.random.normal(k, shape, f32) * sc
    dt_init = jnp.exp(jax.random.uniform(ks[8], (DEPTH, SSM_HEADS), f32, math.log(1e-3), math.log(1e-1)))
    return {
        "x": jax.random.normal(ks[0], (BATCH, SEQ, D_MODEL), f32),
        "norm_mix_w": 1.0 + nrm(ks[1], (DEPTH, D_MODEL), 0.02),
        "w_in": nrm(ks[2], (DEPTH, D_MODEL, IN_WIDTH), D_MODEL ** -0.5),
        "lambda_q1": nrm(ks[3], (DEPTH, ATT_QK_DIM), 0.1),
        "lambda_k1": nrm(ks[4], (DEPTH, ATT_QK_DIM), 0.1),
        "lambda_q2": nrm(ks[5], (DEPTH, ATT_QK_DIM), 0.1),
        "lambda_k2": nrm(ks[6], (DEPTH, ATT_QK_DIM), 0.1),
        "subln_w": 1.0 + nrm(ks[7], (DEPTH, ATT_V_DIM), 0.02),
        "conv_w": jax.random.uniform(ks[9], (DEPTH, SSM_CONV, SSM_CONV_DIM), f32, -0.5, 0.5),
        "conv_b": nrm(ks[10], (DEPTH, SSM_CONV_DIM), 0.01),
        "dt_bias": dt_init + jnp.log(-jnp.expm1(-dt_init)),
        "a_log": jnp.log(jax.random.uniform(ks[11], (DEPTH, SSM_HEADS), f32, 1.0, 16.0)),
        "d_skip": 1.0 + nrm(ks[12], (DEPTH, SSM_HEADS), 0.01),
        "ssm_norm_w": 1.0 + nrm(ks[13], (DEPTH, SSM_WIDTH), 0.02),
        "w_out": nrm(ks[14], (DEPTH, MIX_WIDTH, D_MODEL), MIX_WIDTH ** -0.5),
        "norm_ffn_w": 1.0 + nrm(ks[15], (DEPTH, D_MODEL), 0.02),
        "w_gate": nrm(ks[16], (DEPTH, D_MODEL, D_FF), D_MODEL ** -0.5),
        "w_up": nrm(ks[17], (DEPTH, D_MODEL, D_FF), D_MODEL ** -0.5),
        "w_down": nrm(ks[18], (DEPTH, D_FF, D_MODEL), D_FF ** -0.5),
        "norm_final_w": 1.0 + nrm(ks[19], (D_MODEL,), 0.02),
    }


def reference(x, norm_mix_w, w_in, lambda_q1, lambda_k1, lambda_q2, lambda_k2, subln_w,
              conv_w, conv_b, dt_bias, a_log, d_skip, ssm_norm_w, w_out,
              norm_ffn_w, w_gate, w_up, w_down, norm_final_w):
    b, s = x.shape[0], x.shape[1]
    offs = np.cumsum([ATT_Q_WIDTH, ATT_Q_WIDTH, ATT_WIDTH, SSM_WIDTH, SSM_CONV_DIM]).tolist()
    for layer in range(DEPTH):
        lambda_init = 0.8 - 0.6 * math.exp(-0.3 * layer)
        h = rmsnorm(x, norm_mix_w[layer])
        proj = h @ w_in[layer]
        q, k, v, z, xbc, dt_raw = jnp.split(proj, offs, axis=-1)
        q = q.reshape(b, s, ATT_HEADS, 2, ATT_QK_DIM)
        k = k.reshape(b, s, ATT_HEADS, 2, ATT_QK_DIM)
        v = v.reshape(b, s, ATT_HEADS, ATT_V_DIM)
        lam = (jnp.exp(jnp.sum(lambda_q1[layer].astype(jnp.float32) * lambda_k1[layer].astype(jnp.float32)))
               - jnp.exp(jnp.sum(lambda_q2[layer].astype(jnp.float32) * lambda_k2[layer].astype(jnp.float32)))
               + lambda_init)
        att = diff_attention(q, k, v, lam, subln_w[layer], lambda_init)
        ssm = ssd_mixer(z, xbc, dt_raw, conv_w[layer], conv_b[layer], dt_bias[layer],
                        a_log[layer], d_skip[layer], ssm_norm_w[layer])
        x = x + jnp.concatenate([att, ssm], axis=-1) @ w_out[layer]
        h = rmsnorm(x, norm_ffn_w[layer])
        x = x + (jax.nn.silu(h @ w_gate[layer]) * (h @ w_up[layer])) @ w_down[layer]
    return rmsnorm(x, norm_final_w)
```

```python
import contextlib
import math
import numpy as np
import concourse.bass as bass
import concourse.mybir as mybir
from concourse.bass_utils import run_bass_kernel_spmd

F32 = mybir.dt.float32
BF16 = mybir.dt.bfloat16
AF = mybir.ActivationFunctionType
ALU = mybir.AluOpType

D = 2048
SEQ = 4096
NH = 8
INW = 11296
DFF = 5632
OFF_Q, OFF_K, OFF_V, OFF_Z, OFF_XBC, OFF_DT = 0, 2048, 4096, 6144, 8192, 11264
SCALE = 128 ** -0.5
NEG = -30000.0
SAME_ENG_SYNC = True


class Prog:
    ENGS = ("pe", "act", "dve", "pool", "sp")

    def __init__(self, nc, stack):
        self.nc = nc
        self.stack = stack
        self.ops = {e: [] for e in self.ENGS}
        self.sems = {}
        self.count = {}
        self.last_write = {}
        self.reads_since = {}
        self.seen = {e: {} for e in self.ENGS}
        self.emitted = {e: 0 for e in self.ENGS}
        for e in self.ENGS:
            self._sem("eng_" + e)

    def _sem(self, name):
        if name not in self.sems:
            self.sems[name] = self.stack.enter_context(self.nc.semaphore(name))
            self.count[name] = 0
        return self.sems[name]

    def op(self, eng, fn, reads=(), writes=(), dsem=None, pwrites=()):
        if dsem is not None:
            sname = "d_" + dsem
            self._sem(sname)
            step = 16
        else:
            sname = "eng_" + eng
            step = 1
        self.count[sname] += step
        tok = (sname, self.count[sname])
        deps = {}

        def add(t):
            if t is None:
                return
            s, v = t
            if dsem is None and s == "eng_" + eng and (eng == "pe" or not SAME_ENG_SYNC):
                return
            if deps.get(s, 0) < v:
                deps[s] = v
        for k in list(reads) + list(writes):
            for t in self.last_write.get(k, {}).items():
                add(t)
        for k in list(writes) + list(pwrites):
            for t in self.reads_since.get(k, ()):
                add(t)
        waits = []
        seen = self.seen[eng]
        for s, v in deps.items():
            if seen.get(s, 0) < v:
                seen[s] = v
                waits.append((s, v))
        for k in writes:
            self.last_write[k] = {tok[0]: tok[1]}
            self.reads_since[k] = []
        for k in pwrites:
            self.last_write.setdefault(k, {})[tok[0]] = tok[1]
            self.reads_since[k] = []
        for k in reads:
            self.reads_since.setdefault(k, []).append(tok)
        self.ops[eng].append((waits, fn, sname, step))
        return tok

    def final_wait(self, eng="sp"):
        waits = [(s, v) for s, v in self.count.items() if v > 0 and s != "eng_" + eng]
        self.ops[eng].append((waits, None, None, 0))

    def emit(self):
        nc = self.nc
        with nc.Block() as block:
            for e, deco in (("pe", block.tensor), ("act", block.scalar), ("dve", block.vector),
                            ("pool", block.gpsimd), ("sp", block.sync)):
                ops = self.ops[e][self.emitted[e]:]
                self.emitted[e] = len(self.ops[e])

                def body(engobj, ops=ops):
                    for waits, fn, sname, step in ops:
                        for s, v in waits:
                            engobj.wait_ge(self.sems[s], v)
                        if fn is not None:
                            ins = fn(engobj)
                            ins.then_inc(self.sems[sname], step)
                deco(body)


def fbc(ap, n):
    return bass.AP(ap.tensor, ap.offset, [list(ap.ap[0]), list(ap.ap[1]), [0, n]])


def mbc(ap, m):
    return bass.AP(ap.tensor, ap.offset, [list(ap.ap[0]), [0, m], list(ap.ap[1])])


def pbc(t, n, off=0):
    return bass.AP(t, off, [[0, 128], [1, n]])


def build_program():
    nc = bass.Bass("TRN2", target_bir_lowering=False)
    din = lambda name, shape, dt=F32: nc.dram_tensor(name, shape, dt, kind="ExternalInput")
    xctx = din("xctx", [SEQ, D])
    vmask_d = din("vmask", [128, 32])
    kbias_d = din("kbias", [128, 512])
    w_in = din("w_in", [D, INW])
    w_out = din("w_out", [4096, D])
    w_gate = din("w_gate", [D, DFF])
    w_up = din("w_up", [D, DFF])
    w_down = din("w_down", [DFF, D])
    nmixT_d = din("nmixT", [128, 16])
    nffn_d = din("norm_ffn_w", [1, D])
    nfin_d = din("norm_final_w", [1, D])
    lam_d = din("lam4", [4, 128])
    subw_d = din("subln_w", [1, 256])
    cw_d = din("cwT", [128, 24 * 4])
    cb_d = din("cbT", [128, 24])
    dtb_d = din("dt_bias", [1, 32])
    alog_d = din("a_log", [1, 32])
    dsk_d = din("d_skip", [1, 32])
    snwT_d = din("snwT", [128, 16])
    cst_d = din("cst", [128, 4 * 128])
    abias_d = din("abias", [NH, 128, 5 * 512])
    out_d = nc.dram_tensor("out", [1024, D], F32, kind="ExternalOutput")
    dint = lambda name, shape, dt: nc.dram_tensor(name, shape, dt, kind="Internal")
    kT_s = dint("kT_s", [2048, SEQ], BF16)
    v_s = dint("v_s", [SEQ, 2048], BF16)
    qT_s = dint("qT_s", [2048, 1024], BF16)
    mixT_s = dint("mixT_s", [4096, 1024], BF16)
    x1_s = dint("x1_s", [1024, D], F32)
    h2T_s = dint("h2T_s", [2048, 1024], BF16)

    with contextlib.ExitStack() as st:
        P = Prog(nc, st)
        psum = lambda name, shape, dt: st.enter_context(nc.psum_tensor(name, shape, dt))
        psA = psum("psA", [128, 512], F32)
        psB = psum("psB", [128, 512], F32)
        psC = psum("psC", [128, 512], F32)
        psD = psum("psD", [128, 512], F32)
        psM = psum("psM", [128, 512], F32)
        psY = psum("psY", [128, 512], F32)
        psT0 = psum("psT0", [128, 1024], BF16)
        psT1 = psum("psT1", [128, 1024], BF16)
        rot = {"i": 0}
        accs = [(psA, "psA"), (psB, "psB")]

        def next_acc():
            rot["i"] += 1
            return accs[rot["i"] % 2]
        trot = {"i": 0}
        trs = [(psT0, "psT0"), (psT1, "psT1")]

        def next_tr():
            trot["i"] += 1
            return trs[trot["i"] % len(trs)]

        with contextlib.ExitStack() as cs:
            sb = lambda name, shape, dt: cs.enter_context(nc.sbuf_tensor(name, shape, dt))
            cstf = sb("cstf", [128, 512], F32)
            identb = sb("identb", [128, 128], BF16)
            onescol = sb("onescol", [128, 1], F32)
            P.op("sp", lambda e: e.dma_start(out=cstf[:], in_=cst_d.ap()), writes=["cstf"], dsem="c0")
            P.op("dve", lambda e: e.tensor_copy(out=identb[:], in_=cstf[:, 0:128]), reads=["cstf"], writes=["identb"])
            P.op("dve", lambda e: e.memset(onescol[:], 1.0), writes=["onescol"])
            tri = cstf[:, 128:256]
            U = cstf[:, 256:384]
            ones = cstf[:, 384:512]

            def rms_rstd(xt_ap, xkey, n, eps, junk, ss, rs, tag, jkey=None):
                P.op("act", lambda e: e.activation(out=junk, in_=xt_ap, func=AF.Square, accum_out=ss),
                     reads=[xkey], writes=[tag + "ss", jkey or (tag + "junk")])
                P.op("dve", lambda e: e.tensor_scalar(out=rs, in0=ss, scalar1=1.0 / n, scalar2=eps, op0=ALU.mult, op1=ALU.add),
                     reads=[tag + "ss"], writes=[tag + "rs"])
                P.op("act", lambda e: e.activation(out=rs, in_=rs, func=AF.Sqrt), reads=[tag + "rs"], writes=[tag + "rs"])
                P.op("dve", lambda e: e.reciprocal(out=rs, in_=rs), reads=[tag + "rs"], writes=[tag + "rs"])

            def transposes_to(src_tile, src_key, nchunk, dst_fn, dst_keys, evac_eng="act"):
                for g0 in range(0, nchunk, 8):
                    n = min(8, nchunk - g0)
                    pt, pk = next_tr()

                    def fn(e, g0=g0, n=n, pt=pt):
                        ins = None
                        for i in range(n):
                            ins = e.transpose(out=pt[:, i * 128:(i + 1) * 128], in_=src_tile[:, (g0 + i) * 128:(g0 + i + 1) * 128], identity=identb[:])
                        return ins
                    P.op("pe", fn, reads=[src_key, "identb"], writes=[pk])
                    dst_fn(g0, n, pt, pk)

            def interleave(threads):
                threads = [t for t in threads if t is not None]
                while threads:
                    for t in list(threads):
                        try:
                            next(t)
                        except StopIteration:
                            threads.remove(t)

            with contextlib.ExitStack() as a1:
                sb = lambda name, shape, dt: a1.enter_context(nc.sbuf_tensor(name, shape, dt))
                nmixT = sb("nmixT_sb", [128, 16], F32)
                snwT = sb("snwT_sb", [128, 16], F32)
                xt = [sb(f"xt{i}", [128, D], F32) for i in range(2)]
                hb = [sb(f"hb{i}", [128, D], BF16) for i in range(2)]
                hT = [sb(f"hT{i}", [128, 16, 512], BF16) for i in range(2)]
                NW = 3
                wt = [sb(f"wt{i}", [128, 16, 512], BF16) for i in range(NW)]
                small = sb("small", [128, 64], F32)
                vmask = sb("vmask_sb", [128, 32], F32)
                cw = sb("cw", [128, 96], F32)
                cb = sb("cb", [128, 24], F32)
                halo = sb("halo", [128, 24, 3], F32)
                raw = [sb(f"raw{i}", [128, 515], F32) for i in range(2)]
                cacc = [sb(f"cacc{i}", [128, 512], F32) for i in range(2)]
                cfm = [sb(f"cfm{i}", [128, 512], BF16) for i in range(4)]
                Xtok = sb("Xtok", [128, 4, 2048], BF16)
                Btok = sb("Btok", [128, 4, 512], BF16)
                BT = sb("BT", [128, 4, 512], BF16)
                CT = sb("CT", [128, 4, 512], BF16)
                Hs = sb("Hs", [128, 2048], F32)
                Hb = sb("Hb", [128, 2048], BF16)
                dtb = sb("dtb", [128, 32], F32)
                Abc = sb("Abc", [128, 32], F32)
                dsk = sb("dsk", [128, 32], F32)
                dtt = sb("dtt", [128, 4, 32], F32)
                adt = sb("adt", [128, 4, 32], F32)
                sm2 = [sb(f"sm2_{i}", [128, 6, 32], F32) for i in range(2)]
                Xd = sb("Xd", [128, 2048], BF16)
                kst = [sb(f"kst{i}", [128, 512], BF16) for i in range(2)]
                vst = [sb(f"vst{i}", [128, 512], BF16) for i in range(2)]
                zs = sb("zs", [128, 4, 2048], BF16)
                rseg = sb("rseg", [128, 1024], F32)
                Lt = sb("Lt", [128, 8, 128], F32)
                CBm = sb("CBm", [128, 128], F32)
                Mt = sb("Mt", [128, 8, 128], BF16)
                yb = sb("yb", [128, 512], F32)
                yo = sb("yo", [128, 512], BF16)
                mst = [sb(f"mst{i}", [128, 4, 128], BF16) for i in range(2)]
                Htmp = sb("Htmp", [128, 512], F32)

                ld = lambda eng, out, in_, key, dsem: P.op(eng, lambda e: e.dma_start(out=out, in_=in_), writes=[key], dsem=dsem)
                ld("sp", nmixT[:], nmixT_d.ap(), "nmixT", "c1")
                ld("sp", vmask[:], vmask_d.ap(), "vmask", "c2")
                ld("sp", cw[:], cw_d.ap(), "cw", "c3")
                ld("sp", cb[:], cb_d.ap(), "cb", "c4")
                ld("sp", dtb[:], pbc(dtb_d, 32), "dtb", "c5")
                ld("sp", Abc[:], pbc(alog_d, 32), "Abc", "c6")
                ld("sp", dsk[:], pbc(dsk_d, 32), "dsk", "c7")
                ld("sp", snwT[:], snwT_d.ap(), "snwT", "c8")
                P.op("act", lambda e: e.activation(out=Abc[:], in_=Abc[:], func=AF.Exp), reads=["Abc"], writes=["Abc"])
                P.op("dve", lambda e: e.tensor_scalar(out=Abc[:], in0=Abc[:], scalar1=-1.0, scalar2=None, op0=ALU.mult), reads=["Abc"], writes=["Abc"])
                P.op("dve", lambda e: e.memset(halo[:], 0.0), writes=["halo"])
                P.op("dve", lambda e: e.memset(Hs[:], 0.0), writes=[f"Hs{g}" for g in range(4)])
                P.op("dve", lambda e: e.memset(Hb[:], 0.0), writes=[f"Hb{g}" for g in range(4)])

                wrot = {"i": 0}

                def load_w(c0, ncols):
                    wrot["i"] += 1
                    b = wrot["i"] % NW
                    src = w_in.ap().rearrange("(kc p) n -> p kc n", p=128)[:, :, c0:c0 + ncols]
                    P.op("pool", lambda e: e.dma_start(out=wt[b][:, :, 0:ncols], in_=src), writes=[f"wt{b}"], dsem=f"wt{b}")
                    return wt[b], f"wt{b}"

                def fm_group(w, wk, j, hp):
                    acc, ak = next_acc()

                    def fn(e):
                        ins = None
                        for kc in range(16):
                            ins = e.matmul(acc[:], lhsT=w[:, kc, j * 128:(j + 1) * 128], rhs=hT[hp][:, kc, :], start=(kc == 0), stop=(kc == 15))
                        return ins
                    P.op("pe", fn, reads=[wk, f"hT{hp}"], writes=[ak])
                    return acc, ak

                def tm_group(w, wk, tt, ncols, hp):
                    acc, ak = next_acc()

                    def fn(e):
                        ins = None
                        for kc in range(16):
                            ins = e.matmul(acc[:, 0:ncols], lhsT=hT[hp][:, kc, tt * 128:(tt + 1) * 128], rhs=w[:, kc, 0:ncols], start=(kc == 0), stop=(kc == 15))
                        return ins
                    P.op("pe", fn, reads=[wk, f"hT{hp}"], writes=[ak])
                    return acc, ak

                strot = {"k": 0, "v": 0, "r": 0, "m": 0, "c": 0, "s": 0}

                def th_norm(blk):
                    hp = blk % 2

                    def load(tt):
                        t = blk * 4 + tt
                        b = t % 2
                        P.op("sp", lambda e: e.dma_start(out=xt[b][:], in_=xctx.ap()[t * 128:(t + 1) * 128, :]), writes=[f"xt{b}"], dsem=f"xt{b}")

                    def stats(tt):
                        t = blk * 4 + tt
                        b = t % 2
                        rms_rstd(xt[b][:], f"xt{b}", D, 1e-6, hb[b][:], small[:, 0:1], small[:, 1:2], "n1", jkey=f"hb{b}")
                        P.op("dve", lambda e: e.tensor_scalar(out=hb[b][:], in0=xt[b][:], scalar1=small[:, 1:2], scalar2=None, op0=ALU.mult),
                             reads=[f"xt{b}", "n1rs"], writes=[f"hb{b}"])

                    def trans(tt):
                        t = blk * 4 + tt
                        b = t % 2

                        def dst(g0, n, pt, pk):
                            P.op("dve", lambda e: e.tensor_tensor(out=hT[hp][:, g0:g0 + n, tt * 128:(tt + 1) * 128],
                                                               in0=pt[:, 0:n * 128].rearrange("p (a b) -> p a b", a=n), in1=fbc(nmixT[:, g0:g0 + n], 128), op=ALU.mult),
                                 reads=[pk, "nmixT"], writes=[f"hT{hp}"])
                        transposes_to(hb[b], f"hb{b}", 16, dst, None)
                    load(0)
                    load(1)
                    yield
                    yield
                    for tt in range(4):
                        stats(tt)
                        if tt + 2 < 4:
                            load(tt + 2)
                        yield
                        yield
                        yield
                        if tt >= 1:
                            trans(tt - 1)
                            yield
                    yield
                    yield
                    trans(3)
                    yield

                def th_kvq(blk):
                    hp = blk % 2
                    own = blk >= 6
                    ob = blk - 6
                    for cg in range(4):
                        w, wk = load_w(OFF_K + cg * 512, 512)
                        for j in range(4):
                            acc, ak = fm_group(w, wk, j, hp)
                            strot["k"] += 1
                            sbi = strot["k"] % 2
                            P.op("act", lambda e, acc=acc, sbi=sbi: e.activation(out=kst[sbi][:], in_=acc[:], func=AF.Copy), reads=[ak], writes=[f"kst{sbi}"])
                            row = (cg * 4 + j) * 128
                            P.op("sp", lambda e, sbi=sbi, row=row: e.dma_start(out=kT_s.ap()[row:row + 128, blk * 512:(blk + 1) * 512], in_=kst[sbi][:]),
                                 reads=[f"kst{sbi}"], pwrites=["kT_s"], dsem=f"kst{sbi}")
                            yield
                    for cg in range(4):
                        w, wk = load_w(OFF_V + cg * 512, 512)
                        for tt in range(4):
                            acc, ak = tm_group(w, wk, tt, 512, hp)
                            strot["v"] += 1
                            sbi = strot["v"] % 2
                            P.op("dve", lambda e, acc=acc, sbi=sbi: e.tensor_copy(out=vst[sbi][:], in_=acc[:]), reads=[ak], writes=[f"vst{sbi}"])
                            r0 = (blk * 4 + tt) * 128
                            P.op("sp", lambda e, sbi=sbi, r0=r0, cg=cg: e.dma_start(out=v_s.ap()[r0:r0 + 128, cg * 512:(cg + 1) * 512], in_=vst[sbi][:]),
                                 reads=[f"vst{sbi}"], pwrites=["v_s"], dsem=f"vst{sbi}")
                            yield
                    if own:
                        for cg in range(4):
                            w, wk = load_w(OFF_Q + cg * 512, 512)
                            for j in range(4):
                                acc, ak = fm_group(w, wk, j, hp)
                                strot["k"] += 1
                                sbi = strot["k"] % 2
                                P.op("act", lambda e, acc=acc, sbi=sbi: e.activation(out=kst[sbi][:], in_=acc[:], func=AF.Copy), reads=[ak], writes=[f"kst{sbi}"])
                                row = (cg * 4 + j) * 128
                                P.op("sp", lambda e, sbi=sbi, row=row: e.dma_start(out=qT_s.ap()[row:row + 128, ob * 512:(ob + 1) * 512], in_=kst[sbi][:]),
                                     reads=[f"kst{sbi}"], pwrites=["qT_s"], dsem=f"kst{sbi}")
                                yield

                def xbc_dt_z(blk):
                    hp = blk % 2
                    own = blk >= 6
                    if own:
                        for cg in range(4):
                            w, wk = load_w(OFF_Z + cg * 512, 512)
                            for tt in range(4):
                                acc, ak = tm_group(w, wk, tt, 512, hp)
                                P.op("act", lambda e, acc=acc, tt=tt, cg=cg: e.activation(out=zs[:, tt, cg * 512:(cg + 1) * 512], in_=acc[:], func=AF.Silu),
                                     reads=[ak], writes=["zs"])
                    w, wk = load_w(OFF_DT, 32)
                    for tt in range(4):
                        t = blk * 4 + tt
                        acc, ak = tm_group(w, wk, tt, 32, hp)
                        P.op("dve", lambda e, acc=acc, tt=tt: e.tensor_tensor(out=dtt[:, tt, :], in0=acc[:, 0:32], in1=dtb[:], op=ALU.add), reads=[ak, "dtb"], writes=["dtt"])
                        P.op("act", lambda e, tt=tt: e.activation(out=dtt[:, tt, :], in_=dtt[:, tt, :], func=AF.Exp), reads=["dtt"], writes=["dtt"])
                        P.op("act", lambda e, tt=tt: e.activation(out=dtt[:, tt, :], in_=dtt[:, tt, :], func=AF.Ln, bias=onescol[:, 0:1], scale=1.0), reads=["dtt", "onescol"], writes=["dtt"])
                        P.op("dve", lambda e, tt=tt, t=t: e.tensor_scalar(out=dtt[:, tt, :], in0=dtt[:, tt, :], scalar1=vmask[:, t:t + 1], scalar2=None, op0=ALU.mult), reads=["dtt", "vmask"], writes=["dtt"])
                        P.op("dve", lambda e, tt=tt: e.tensor_tensor(out=adt[:, tt, :], in0=dtt[:, tt, :], in1=Abc[:], op=ALU.mult), reads=["dtt", "Abc"], writes=["adt"])
                    pending = []

                    def flush(upto):
                        while len(pending) > upto:
                            pending.pop(0)()
                    for cg in range(6 if blk >= 5 else 5):
                        w, wk = load_w(OFF_XBC + cg * 512, 512)
                        for j in range(4):
                            ci = cg * 4 + j
                            acc, ak = fm_group(w, wk, j, hp)
                            strot["r"] += 1
                            rb = strot["r"] % 2
                            P.op("act", lambda e, acc=acc, rb=rb: e.activation(out=raw[rb][:, 3:515], in_=acc[:], func=AF.Copy), reads=[ak], writes=[f"raw{rb}"])
                            P.op("act", lambda e, rb=rb, ci=ci: e.activation(out=raw[rb][:, 0:3], in_=halo[:, ci, :], func=AF.Copy), reads=["halo"], writes=[f"raw{rb}"])
                            P.op("dve", lambda e, rb=rb, ci=ci: e.tensor_scalar(out=cacc[rb][:], in0=raw[rb][:, 3:515], scalar1=cw[:, ci * 4 + 3:ci * 4 + 4], scalar2=cb[:, ci:ci + 1], op0=ALU.mult, op1=ALU.add),
                                 reads=[f"raw{rb}", "cw", "cb"], writes=[f"cacc{rb}"])
                            for k in range(3):
                                P.op("dve", lambda e, rb=rb, ci=ci, k=k: e.scalar_tensor_tensor(out=cacc[rb][:], in0=raw[rb][:, k:k + 512], scalar=cw[:, ci * 4 + k:ci * 4 + k + 1], in1=cacc[rb][:], op0=ALU.mult, op1=ALU.add),
                                     reads=[f"raw{rb}", "cw", f"cacc{rb}"], writes=[f"cacc{rb}"])
                            P.op("act", lambda e, rb=rb, ci=ci: e.activation(out=halo[:, ci, :], in_=raw[rb][:, 512:515], func=AF.Copy), reads=[f"raw{rb}"], writes=["halo"])
                            if ci < 16:
                                strot["c"] += 1
                                cf = strot["c"] % 4
                                P.op("act", lambda e, rb=rb, cf=cf: e.activation(out=cfm[cf][:], in_=cacc[rb][:], func=AF.Silu), reads=[f"cacc{rb}"], writes=[f"cfm{cf}"])

                                def later(ci=ci, cf=cf):
                                    def dst(g0, n, pt, pk):
                                        P.op("dve", lambda e: e.tensor_copy(out=Xtok[:, :, ci * 128:(ci + 1) * 128], in_=pt[:, 0:512].rearrange("p (a b) -> p a b", a=4)),
                                             reads=[pk], writes=["Xtok"])
                                    transposes_to(cfm[cf], f"cfm{cf}", 4, dst, None)
                                pending.append(later)
                            elif ci < 20:
                                g = ci - 16
                                P.op("act", lambda e, rb=rb, g=g: e.activation(out=BT[:, g, :], in_=cacc[rb][:], func=AF.Silu), reads=[f"cacc{rb}"], writes=[f"BT{g}"])

                                def later(g=g):
                                    def dst(g0, n, pt, pk):
                                        P.op("dve", lambda e: e.tensor_copy(out=Btok[:, :, g * 128:(g + 1) * 128], in_=pt[:, 0:512].rearrange("p (a b) -> p a b", a=4)),
                                             reads=[pk], writes=["Btok"])
                                    transposes_to(BT[:, g, :], f"BT{g}", 4, dst, None)
                                pending.append(later)
                            else:
                                g = ci - 20
                                P.op("act", lambda e, rb=rb, g=g: e.activation(out=CT[:, g, :], in_=cacc[rb][:], func=AF.Silu), reads=[f"cacc{rb}"], writes=[f"CT{g}"])
                            flush(2)
                    return pending

                def th_ssd(blk):
                    own = blk >= 6
                    ob = blk - 6
                    for tt in range(4):
                        ot = ob * 4 + tt
                        strot["s"] += 1
                        sm = sm2[strot["s"] % 2]
                        smk = f"sm{strot['s'] % 2}"

                        def fn(e, tt=tt):
                            e.matmul(psM[:, 0:32], lhsT=tri, rhs=adt[:, tt, :], start=True, stop=True)
                            return e.matmul(psM[:, 32:64], lhsT=ones, rhs=adt[:, tt, :], start=True, stop=True)
                        P.op("pe", fn, reads=["adt", "cstf"], writes=["psM"])
                        yield
                        dd, dsd, cd, ea = sm[:, 0, :], sm[:, 1, :], sm[:, 2, :], sm[:, 3, :]
                        P.op("dve", lambda e, sm=sm: e.tensor_copy(out=sm[:, 4, :], in_=psM[:, 0:32]), reads=["psM"], writes=[smk + "acol"])
                        P.op("dve", lambda e, sm=sm, dd=dd: e.tensor_tensor(out=dd, in0=psM[:, 32:64], in1=sm[:, 4, :], op=ALU.subtract), reads=["psM", smk + "acol"], writes=[smk + "dd"])
                        P.op("act", lambda e, dsd=dsd, dd=dd: e.activation(out=dsd, in_=dd, func=AF.Exp), reads=[smk + "dd"], writes=[smk + "dsd"])
                        P.op("act", lambda e, cd=cd: e.activation(out=cd, in_=psM[:, 32:64], func=AF.Exp), reads=["psM"], writes=[smk + "cd"])
                        if own:
                            P.op("act", lambda e, ea=ea: e.activation(out=ea, in_=psM[:, 0:32], func=AF.Exp), reads=["psM"], writes=[smk + "ea"])
                        P.op("dve", lambda e, tt=tt, dsd=dsd: e.tensor_tensor(out=dsd, in0=dsd, in1=dtt[:, tt, :], op=ALU.mult), reads=[smk + "dsd", "dtt"], writes=[smk + "dsd"])
                        P.op("dve", lambda e, tt=tt, dsd=dsd: e.tensor_tensor(out=Xd[:].rearrange("p (a b) -> p a b", a=32), in0=Xtok[:, tt, :].rearrange("p (a b) -> p a b", a=32), in1=fbc(dsd, 64), op=ALU.mult),
                             reads=["Xtok", smk + "dsd"], writes=["Xd"])
                        yield
                        for g in range(4):
                            gs = slice(g * 512, (g + 1) * 512)
                            y3 = lambda ap: ap.rearrange("p (a b) -> p a b", a=8)
                            if own:
                                P.op("dve", lambda e, tt=tt, g=g: e.tensor_tensor(out=rseg[:].rearrange("p (a b) -> p a b", a=8), in0=mbc(tri, 8), in1=fbc(adt[:, tt, g * 8:(g + 1) * 8], 128), op=ALU.mult),
                                     reads=["adt", "cstf"], writes=["rseg"])
                                P.op("pe", lambda e, tt=tt, g=g: e.matmul(psM[:, 128:256], lhsT=BT[:, g, tt * 128:(tt + 1) * 128], rhs=CT[:, g, tt * 128:(tt + 1) * 128], start=True, stop=True),
                                     reads=[f"BT{g}", f"CT{g}"], writes=["psM"])
                                P.op("dve", lambda e: e.tensor_tensor(out=CBm[:], in0=psM[:, 128:256], in1=tri, op=ALU.mult), reads=["psM", "cstf"], writes=["CBm"])
                                acc, ak = psD, "psD"
                                P.op("pe", lambda e, acc=acc, tt=tt, g=g, gs=gs: e.matmul(acc[:], lhsT=CT[:, g, tt * 128:(tt + 1) * 128], rhs=Hb[:, gs], start=True, stop=True),
                                     reads=[f"CT{g}", f"Hb{g}"], writes=[ak])
                                P.op("dve", lambda e, acc=acc, g=g, ea=ea: e.tensor_tensor(out=y3(yb[:]), in0=y3(acc[:]), in1=fbc(ea[:, g * 8:(g + 1) * 8], 64), op=ALU.mult), reads=[ak, smk + "ea"], writes=["yb"])
                                P.op("dve", lambda e, tt=tt, g=g, gs=gs: e.tensor_tensor(out=y3(Htmp[:]), in0=y3(Xtok[:, tt, gs]), in1=fbc(dsk[:, g * 8:(g + 1) * 8], 64), op=ALU.mult), reads=["Xtok", "dsk"], writes=["Htmp"])
                                yield

                                P.op("pe", lambda e: e.matmul(psC[:], lhsT=U, rhs=rseg[:, 0:512], start=True, stop=True), reads=["rseg", "cstf"], writes=["psC"])
                                P.op("act", lambda e: e.activation(out=Lt[:, 0:4, :], in_=psC[:].rearrange("p (a b) -> p a b", a=4), func=AF.Exp), reads=["psC"], writes=["Lt0"])
                                P.op("pe", lambda e: e.matmul(psC[:], lhsT=U, rhs=rseg[:, 512:1024], start=True, stop=True), reads=["rseg", "cstf"], writes=["psC"])
                                P.op("act", lambda e: e.activation(out=Lt[:, 4:8, :], in_=psC[:].rearrange("p (a b) -> p a b", a=4), func=AF.Exp), reads=["psC"], writes=["Lt1"])
                                P.op("dve", lambda e: e.tensor_tensor(out=Lt[:], in0=Lt[:], in1=mbc(CBm[:], 8), op=ALU.mult), reads=["Lt0", "Lt1", "CBm"], writes=["Lt0", "Lt1"])
                                P.op("dve", lambda e, tt=tt, g=g: e.tensor_tensor(out=Mt[:], in0=Lt[:], in1=fbc(dtt[:, tt, g * 8:(g + 1) * 8], 128), op=ALU.mult), reads=["Lt0", "Lt1", "dtt"], writes=["Mt"])
                                yield
                            acc2, ak2 = (psD, "psD") if (own or g % 2) else (psC, "psC")
                            P.op("pe", lambda e, acc2=acc2, tt=tt, g=g, gs=gs: e.matmul(acc2[:], lhsT=Btok[:, tt, g * 128:(g + 1) * 128], rhs=Xd[:, gs], start=True, stop=True),
                                 reads=["Btok", "Xd"], writes=[ak2])
                            P.op("dve", lambda e, g=g, gs=gs, cd=cd: e.tensor_tensor(out=y3(Hs[:, gs]), in0=y3(Hs[:, gs]), in1=fbc(cd[:, g * 8:(g + 1) * 8], 64), op=ALU.mult), reads=[f"Hs{g}", smk + "cd"], writes=[f"Hs{g}"])
                            P.op("dve", lambda e, acc2=acc2, gs=gs: e.tensor_tensor(out=Hs[:, gs], in0=Hs[:, gs], in1=acc2[:], op=ALU.add), reads=[f"Hs{g}", ak2], writes=[f"Hs{g}"])
                            P.op("act", lambda e, gs=gs: e.activation(out=Hb[:, gs], in_=Hs[:, gs], func=AF.Copy), reads=[f"Hs{g}"], writes=[f"Hb{g}"])
                            yield
                            if own:
                                def fn(e, tt=tt, g=g):
                                    ins = None
                                    for r in range(8):
                                        hh = g * 8 + r
                                        ins = e.matmul(psY[:, r * 64:(r + 1) * 64], lhsT=Mt[:, r, :], rhs=Xtok[:, tt, hh * 64:(hh + 1) * 64], start=True, stop=True)
                                    return ins
                                P.op("pe", fn, reads=["Mt", "Xtok"], writes=["psY"])
                                P.op("dve", lambda e: e.tensor_tensor(out=yb[:], in0=yb[:], in1=psY[:], op=ALU.add), reads=["yb", "psY"], writes=["yb"])
                                P.op("dve", lambda e: e.tensor_tensor(out=yb[:], in0=yb[:], in1=Htmp[:], op=ALU.add), reads=["yb", "Htmp"], writes=["yb"])
                                P.op("dve", lambda e, tt=tt, gs=gs: e.tensor_tensor(out=yb[:], in0=yb[:], in1=zs[:, tt, gs], op=ALU.mult), reads=["yb", "zs"], writes=["yb"])
                                rms_rstd(yb[:], "yb", 512, 1e-5, yo[:], small[:, 2:3], small[:, 3:4], "n2", jkey="yo")
                                P.op("dve", lambda e: e.tensor_scalar(out=yo[:], in0=yb[:], scalar1=small[:, 3:4], scalar2=None, op0=ALU.mult), reads=["yb", "n2rs"], writes=["yo"])
                                yield
                                strot["m"] += 1
                                mb = strot["m"] % 2

                                def dst(g0, n, pt, pk, mb=mb, g=g):
                                    P.op("dve", lambda e: e.tensor_tensor(out=mst[mb][:], in0=pt[:, 0:512].rearrange("p (a b) -> p a b", a=4), in1=fbc(snwT[:, g * 4:(g + 1) * 4], 128), op=ALU.mult),
                                         reads=[pk, "snwT"], writes=[f"mst{mb}"])
                                transposes_to(yo, "yo", 4, dst, None)
                                r0 = 2048 + g * 512
                                dstap = mixT_s.ap()[r0:r0 + 512, ot * 128:(ot + 1) * 128].rearrange("(a p) t -> p a t", p=128)
                                P.op("sp", lambda e, mb=mb, dstap=dstap: e.dma_start(out=dstap, in_=mst[mb][:]), reads=[f"mst{mb}"], pwrites=["mixT_s"], dsem=f"mst{mb}")
                                yield

                def th_ssd_own(blk):
                    ob = blk - 6
                    y3 = lambda ap: ap.rearrange("p (a b) -> p a b", a=8)
                    chunk_sm = {}

                    def chunk_ops(tt):
                        strot["s"] += 1
                        sm = sm2[strot["s"] % 2]
                        smk = f"sm{strot['s'] % 2}"
                        chunk_sm[tt] = (sm, smk)

                        def fn(e):
                            e.matmul(psM[:, 0:32], lhsT=tri, rhs=adt[:, tt, :], start=True, stop=True)
                            return e.matmul(psM[:, 32:64], lhsT=ones, rhs=adt[:, tt, :], start=True, stop=True)
                        P.op("pe", fn, reads=["adt", "cstf"], writes=["psM"])
                        dd, dsd, cd, ea = sm[:, 0, :], sm[:, 1, :], sm[:, 2, :], sm[:, 3, :]
                        P.op("dve", lambda e: e.tensor_copy(out=sm[:, 4, :], in_=psM[:, 0:32]), reads=["psM"], writes=[smk + "acol"])
                        P.op("dve", lambda e: e.tensor_tensor(out=dd, in0=psM[:, 32:64], in1=sm[:, 4, :], op=ALU.subtract), reads=["psM", smk + "acol"], writes=[smk + "dd"])
                        P.op("act", lambda e: e.activation(out=dsd, in_=dd, func=AF.Exp), reads=[smk + "dd"], writes=[smk + "dsd"])
                        P.op("act", lambda e: e.activation(out=cd, in_=psM[:, 32:64], func=AF.Exp), reads=["psM"], writes=[smk + "cd"])
                        P.op("act", lambda e: e.activation(out=ea, in_=psM[:, 0:32], func=AF.Exp), reads=["psM"], writes=[smk + "ea"])
                        P.op("dve", lambda e: e.tensor_tensor(out=dsd, in0=dsd, in1=dtt[:, tt, :], op=ALU.mult), reads=[smk + "dsd", "dtt"], writes=[smk + "dsd"])

                    def xd_op(tt):
                        sm, smk = chunk_sm[tt]
                        dsd = sm[:, 1, :]
                        P.op("dve", lambda e: e.tensor_tensor(out=Xd[:].rearrange("p (a b) -> p a b", a=32), in0=Xtok[:, tt, :].rearrange("p (a b) -> p a b", a=32), in1=fbc(dsd, 64), op=ALU.mult),
                             reads=["Xtok", smk + "dsd"], writes=["Xd"])

                    def head(tt, g):
                        P.op("dve", lambda e: e.tensor_tensor(out=rseg[:].rearrange("p (a b) -> p a b", a=8), in0=mbc(tri, 8), in1=fbc(adt[:, tt, g * 8:(g + 1) * 8], 128), op=ALU.mult),
                             reads=["adt", "cstf"], writes=["rseg"])
                        P.op("pe", lambda e: e.matmul(psM[:, 128:256], lhsT=BT[:, g, tt * 128:(tt + 1) * 128], rhs=CT[:, g, tt * 128:(tt + 1) * 128], start=True, stop=True),
                             reads=[f"BT{g}", f"CT{g}"], writes=["psM"])
                        P.op("dve", lambda e: e.tensor_tensor(out=CBm[:], in0=psM[:, 128:256], in1=tri, op=ALU.mult), reads=["psM", "cstf"], writes=["CBm"])
                        yield
                        P.op("pe", lambda e: e.matmul(psC[:], lhsT=U, rhs=rseg[:, 0:512], start=True, stop=True), reads=["rseg", "cstf"], writes=["psC"])
                        P.op("act", lambda e: e.activation(out=Lt[:, 0:4, :], in_=psC[:].rearrange("p (a b) -> p a b", a=4), func=AF.Exp), reads=["psC"], writes=["Lt0"])
                        acc, ak = next_acc()
                        P.op("pe", lambda e: e.matmul(acc[:], lhsT=U, rhs=rseg[:, 512:1024], start=True, stop=True), reads=["rseg", "cstf"], writes=[ak])
                        P.op("act", lambda e: e.activation(out=Lt[:, 4:8, :], in_=acc[:].rearrange("p (a b) -> p a b", a=4), func=AF.Exp), reads=[ak], writes=["Lt1"])
                        yield
                        P.op("dve", lambda e: e.tensor_tensor(out=Lt[:], in0=Lt[:], in1=mbc(CBm[:], 8), op=ALU.mult), reads=["Lt0", "Lt1", "CBm"], writes=["Lt0", "Lt1"])
                        P.op("dve", lambda e: e.tensor_tensor(out=Mt[:], in0=Lt[:], in1=fbc(dtt[:, tt, g * 8:(g + 1) * 8], 128), op=ALU.mult), reads=["Lt0", "Lt1", "dtt"], writes=["Mt"])
                        yield

                    def tail(tt, g):
                        sm, smk = chunk_sm[tt]
                        cd, ea = sm[:, 2, :], sm[:, 3, :]
                        gs = slice(g * 512, (g + 1) * 512)
                        ot = ob * 4 + tt
                        def fn(e):
                            ins = None
                            for r in range(8):
                                hh = g * 8 + r
                                ins = e.matmul(psY[:, r * 64:(r + 1) * 64], lhsT=Mt[:, r, :], rhs=Xtok[:, tt, hh * 64:(hh + 1) * 64], start=True, stop=True)
                            return ins
                        P.op("pe", fn, reads=["Mt", "Xtok"], writes=["psY"])
                        P.op("pe", lambda e: e.matmul(psD[:], lhsT=CT[:, g, tt * 128:(tt + 1) * 128], rhs=Hb[:, gs], start=True, stop=True),
                             reads=[f"CT{g}", f"Hb{g}"], writes=["psD"])
                        P.op("dve", lambda e: e.tensor_tensor(out=y3(yb[:]), in0=y3(psD[:]), in1=fbc(ea[:, g * 8:(g + 1) * 8], 64), op=ALU.mult), reads=["psD", smk + "ea"], writes=["yb"])
                        P.op("dve", lambda e: e.tensor_tensor(out=y3(Htmp[:]), in0=y3(Xtok[:, tt, gs]), in1=fbc(dsk[:, g * 8:(g + 1) * 8], 64), op=ALU.mult), reads=["Xtok", "dsk"], writes=["Htmp"])
                        yield
                        P.op("pe", lambda e: e.matmul(psD[:], lhsT=Btok[:, tt, g * 128:(g + 1) * 128], rhs=Xd[:, gs], start=True, stop=True),
                             reads=["Btok", "Xd"], writes=["psD"])
                        P.op("dve", lambda e: e.tensor_tensor(out=yb[:], in0=yb[:], in1=psY[:], op=ALU.add), reads=["yb", "psY"], writes=["yb"])
                        P.op("dve", lambda e: e.tensor_tensor(out=y3(Hs[:, gs]), in0=y3(Hs[:, gs]), in1=fbc(cd[:, g * 8:(g + 1) * 8], 64), op=ALU.mult), reads=[f"Hs{g}", smk + "cd"], writes=[f"Hs{g}"])
                        P.op("dve", lambda e: e.tensor_tensor(out=Hs[:, gs], in0=Hs[:, gs], in1=psD[:], op=ALU.add), reads=[f"Hs{g}", "psD"], writes=[f"Hs{g}"])
                        P.op("act", lambda e: e.activation(out=Hb[:, gs], in_=Hs[:, gs], func=AF.Copy), reads=[f"Hs{g}"], writes=[f"Hb{g}"])
                        yield
                        P.op("dve", lambda e: e.tensor_tensor(out=yb[:], in0=yb[:], in1=Htmp[:], op=ALU.add), reads=["yb", "Htmp"], writes=["yb"])
                        P.op("dve", lambda e: e.tensor_tensor(out=yb[:], in0=yb[:], in1=zs[:, tt, gs], op=ALU.mult), reads=["yb", "zs"], writes=["yb"])
                        rms_rstd(yb[:], "yb", 512, 1e-5, yo[:], small[:, 2:3], small[:, 3:4], "n2", jkey="yo")
                        yield
                        P.op("dve", lambda e: e.tensor_scalar(out=yo[:], in0=yb[:], scalar1=small[:, 3:4], scalar2=None, op0=ALU.mult), reads=["yb", "n2rs"], writes=["yo"])
                        yield
                        strot["m"] += 1
                        mb = strot["m"] % 2

                        def dst(g0, n, pt, pk):
                            P.op("dve", lambda e: e.tensor_tensor(out=mst[mb][:], in0=pt[:, 0:512].rearrange("p (a b) -> p a b", a=4), in1=fbc(snwT[:, g * 4:(g + 1) * 4], 128), op=ALU.mult),
                                 reads=[pk, "snwT"], writes=[f"mst{mb}"])
                        transposes_to(yo, "yo", 4, dst, None)
                        r0 = 2048 + g * 512
                        dstap = mixT_s.ap()[r0:r0 + 512, ot * 128:(ot + 1) * 128].rearrange("(a p) t -> p a t", p=128)
                        P.op("sp", lambda e: e.dma_start(out=dstap, in_=mst[mb][:]), reads=[f"mst{mb}"], pwrites=["mixT_s"], dsem=f"mst{mb}")
                        yield

                    units = [(tt, g) for tt in range(4) for g in range(4)]
                    chunk_ops(0)
                    yield
                    xd_op(0)
                    for _ in head(0, 0):
                        yield
                    for ui, (tt, g) in enumerate(units):
                        nxt = units[ui + 1] if ui + 1 < len(units) else None
                        if nxt is not None and nxt[1] == 0:
                            chunk_ops(nxt[0])
                        tl = tail(tt, g)
                        hd = head(*nxt) if nxt is not None else None
                        done_t = done_h = False
                        while not (done_t and (done_h or hd is None)):
                            if not done_t:
                                try:
                                    next(tl)
                                except StopIteration:
                                    done_t = True
                            if hd is not None and not done_h:
                                try:
                                    next(hd)
                                except StopIteration:
                                    done_h = True
                            yield
                        if nxt is not None and nxt[1] == 0:
                            xd_op(nxt[0])

                def th_ssd_w(blk, pend):
                    for _ in range(3):
                        yield
                    while pend:
                        pend.pop(0)()
                        yield
                    yield from (th_ssd_own(blk) if blk >= 6 else th_ssd(blk))

                interleave([th_norm(0)])
                for blk in range(8):
                    pend = xbc_dt_z(blk)
                    interleave([th_kvq(blk), th_ssd_w(blk, pend), th_norm(blk + 1) if blk < 7 else None])
                P.emit()

            with contextlib.ExitStack() as a2:
                sb = lambda name, shape, dt: a2.enter_context(nc.sbuf_tensor(name, shape, dt))
                kT = [sb(f"kT{i}", [128, 2, SEQ], BF16) for i in range(2)]
                vv = [sb(f"vv{i}", [128, 32, 257], BF16) for i in range(2)]
                qT = [sb(f"qT{i}", [128, 2, 1024], BF16) for i in range(2)]
                ab = [sb(f"ab{i}", [128, 5, 512], F32) for i in range(2)]
                kbias = sb("kbias_sb", [128, 512], F32)
                lam = sb("lam_sb", [128, 4, 128], F32)
                lsm = sb("lsm", [128, 8], F32)
                subw = sb("subw", [128, 256], F32)
                NS, NP = 3, 4
                Sb = [sb(f"Sb{i}", [128, 512], F32) for i in range(NS)]
                Pt = [sb(f"Pt{i}", [128, 512], BF16) for i in range(NP)]
                Oraw = sb("Oraw", [128, 8, 257], F32)
                dsm = sb("dsm", [128, 16], F32)
                dtile = sb("dtile", [128, 1024], F32)
                ajunk = sb("ajunk", [128, 256], BF16)
                atb = sb("atb", [128, 1024], BF16)
                ast = [sb(f"ast{i}", [128, 2, 512], BF16) for i in range(2)]
                ld = lambda eng, out, in_, key, dsem: P.op(eng, lambda e: e.dma_start(out=out, in_=in_), writes=[key], dsem=dsem)
                ld("sp", kbias[:], kbias_d.ap(), "kbias", "c1")
                ld("sp", subw[:], pbc(subw_d, 256), "subw", "c2")
                for i in range(4):
                    ld("sp", lam[:, i, :], pbc(lam_d, 128, off=i * 128), f"lam{i}", f"c{3 + i}")
                P.op("dve", lambda e: e.tensor_scalar(out=subw[:], in0=subw[:], scalar1=0.8, scalar2=None, op0=ALU.mult), reads=["subw"], writes=["subw"])
                for i in range(2):
                    P.op("dve", lambda e, i=i: e.tensor_tensor(out=lam[:, 2 * i, :], in0=lam[:, 2 * i, :], in1=lam[:, 2 * i + 1, :], op=ALU.mult), reads=[f"lam{2 * i}", f"lam{2 * i + 1}"], writes=[f"lam{2 * i}"])
                    P.op("act", lambda e, i=i: e.activation(out=lam[:, 2 * i + 1, :], in_=lam[:, 2 * i, :], func=AF.Copy, accum_out=lsm[:, i:i + 1]), reads=[f"lam{2 * i}"], writes=[f"lam{2 * i + 1}", f"lsm{i}"])
                    P.op("act", lambda e, i=i: e.activation(out=lsm[:, 2 + i:3 + i], in_=lsm[:, i:i + 1], func=AF.Exp), reads=[f"lsm{i}"], writes=[f"lse{i}"])
                P.op("dve", lambda e: e.tensor_tensor(out=lsm[:, 4:5], in0=lsm[:, 3:4], in1=lsm[:, 2:3], op=ALU.subtract), reads=["lse0", "lse1"], writes=["neglam"])
                P.op("dve", lambda e: e.tensor_scalar(out=lsm[:, 4:5], in0=lsm[:, 4:5], scalar1=-0.2, scalar2=None, op0=ALU.add), reads=["neglam"], writes=["neglam"])
                for i in range(2):
                    P.op("dve", lambda e, i=i: e.memset(vv[i][:, :, 256:257], 1.0), writes=[f"vv{i}"])
                sbanks = [(psA, "psA"), (psB, "psB"), (psY, "psY")]
                Ob = [(psC, "psC"), (psD, "psD"), (psM, "psM"), (psT1[:].bitcast(F32), "psT1")]
                trs[:] = [(psT0, "psT0")]

                def load_head(h):
                    hb_ = h % 2
                    P.op("sp", lambda e: e.dma_start(out=kT[hb_][:], in_=kT_s.ap()[h * 256:(h + 1) * 256, :].rearrange("(c p) t -> p c t", p=128)),
                         reads=["kT_s"], writes=[f"kT{hb_}"], dsem=f"kT{hb_}")
                    P.op("act", lambda e: e.dma_start(out=vv[hb_][:, :, 0:256], in_=v_s.ap()[:, h * 256:(h + 1) * 256].rearrange("(t p) e -> p t e", p=128)),
                         reads=["v_s"], writes=[f"vv{hb_}"], dsem=f"vv{hb_}")
                    P.op("sp", lambda e: e.dma_start(out=qT[hb_][:], in_=qT_s.ap()[h * 256:(h + 1) * 256, :].rearrange("(c p) t -> p c t", p=128)),
                         reads=["qT_s"], writes=[f"qT{hb_}"], dsem=f"qT{hb_}")
                    P.op("sp", lambda e: e.dma_start(out=ab[hb_][:], in_=abias_d.ap()[h].rearrange("p (a b) -> p a b", a=5)),
                         writes=[f"ab{hb_}"], dsem=f"ab{hb_}")

                tiles = [(h, qb, c, kb) for h in range(NH) for qb in range(2) for c in range(2) for kb in range(28 + 4 * qb)]
                arot = {"a": 0}

                def front(i):
                    h, qb, c, kb = tiles[i]
                    hb_ = h % 2
                    jd = kb - (24 + 4 * qb)
                    acc, ak = sbanks[i % 3]
                    P.op("pe", lambda e: e.matmul(acc[:], lhsT=kT[hb_][:, c, kb * 128:(kb + 1) * 128], rhs=qT[hb_][:, c, qb * 512:(qb + 1) * 512], start=True, stop=True),
                         reads=[f"kT{hb_}", f"qT{hb_}"], writes=[ak])
                    si = i % NS
                    bt = ab[hb_][:, (jd + 1) if jd >= 0 else 0, :]
                    P.op("dve", lambda e: e.tensor_tensor(out=Sb[si][:], in0=acc[:], in1=bt, op=ALU.add), reads=[ak, f"ab{hb_}"], writes=[f"Sb{si}"])
                    pi = i % NP
                    col = h * 64 + qb * 32 + kb
                    P.op("act", lambda e: e.activation(out=Pt[pi][:], in_=Sb[si][:], func=AF.Exp, bias=kbias[:, col:col + 1], scale=SCALE),
                         reads=[f"Sb{si}", "kbias"], writes=[f"Pt{pi}"])

                def back(i):
                    h, qb, c, kb = tiles[i]
                    hb_ = h % 2
                    nkb = 28 + 4 * qb
                    pi = i % NP

                    def fn(e):
                        ins = None
                        for s in range(4):
                            ins = e.matmul(Ob[s][0][:, 0:257], lhsT=Pt[pi][:, s * 128:(s + 1) * 128], rhs=vv[hb_][:, kb, 0:257], start=(kb == 0), stop=(kb == nkb - 1))
                        return ins
                    P.op("pe", fn, reads=[f"Pt{pi}", f"vv{hb_}"], writes=[o[1] for o in Ob])
                    if kb != nkb - 1:
                        return
                    for s in range(4):
                        if s < 2:
                            P.op("act", lambda e, s=s: e.activation(out=Oraw[:, c * 4 + s, :], in_=Ob[s][0][:, 0:257], func=AF.Copy), reads=[Ob[s][1]], writes=[f"Oraw{c}_{s}"])
                        else:
                            P.op("dve", lambda e, s=s: e.tensor_copy(out=Oraw[:, c * 4 + s, :], in_=Ob[s][0][:, 0:257]), reads=[Ob[s][1]], writes=[f"Oraw{c}_{s}"])
                    if c != 1:
                        return

                    ork = [f"Oraw{cc}_{s}" for cc in range(2) for s in range(4)]

                    def e_a():
                        P.op("dve", lambda e: e.reciprocal(out=dsm[:, 0:8], in_=Oraw[:, :, 256]), reads=ork, writes=["dsmr"])
                        P.op("dve", lambda e: e.tensor_scalar(out=dsm[:, 4:8], in0=dsm[:, 4:8], scalar1=lsm[:, 4:5], scalar2=None, op0=ALU.mult), reads=["dsmr", "neglam"], writes=["dsmr"])

                    def e_b(s):
                        def f():
                            P.op("dve", lambda e: e.tensor_scalar(out=dtile[:, s * 256:(s + 1) * 256], in0=Oraw[:, 4 + s, 0:256], scalar1=dsm[:, 4 + s:5 + s], scalar2=None, op0=ALU.mult),
                                 reads=ork + ["dsmr"], writes=[f"dtile{s}"])
                            P.op("dve", lambda e: e.scalar_tensor_tensor(out=dtile[:, s * 256:(s + 1) * 256], in0=Oraw[:, s, 0:256], scalar=dsm[:, s:s + 1], in1=dtile[:, s * 256:(s + 1) * 256], op0=ALU.mult, op1=ALU.add),
                                 reads=ork + ["dsmr", f"dtile{s}"], writes=[f"dtile{s}"])
                        return f

                    def e_c(s):
                        def f():
                            P.op("act", lambda e: e.activation(out=ajunk[:], in_=dtile[:, s * 256:(s + 1) * 256], func=AF.Square, accum_out=dsm[:, 8 + s:9 + s]), reads=[f"dtile{s}"], writes=["ajunk", f"ass{s}"])
                        return f

                    def e_d():
                        P.op("dve", lambda e: e.tensor_scalar(out=dsm[:, 12:16], in0=dsm[:, 8:12], scalar1=1.0 / 256, scalar2=1e-5, op0=ALU.mult, op1=ALU.add), reads=[f"ass{s}" for s in range(4)], writes=["ars"])

                    def e_e():
                        P.op("act", lambda e: e.activation(out=dsm[:, 12:16], in_=dsm[:, 12:16], func=AF.Sqrt), reads=["ars"], writes=["ars"])

                    def e_f():
                        P.op("dve", lambda e: e.reciprocal(out=dsm[:, 12:16], in_=dsm[:, 12:16]), reads=["ars"], writes=["ars"])

                    def e_g(s):
                        def f():
                            P.op("dve", lambda e: e.scalar_tensor_tensor(out=atb[:, s * 256:(s + 1) * 256], in0=dtile[:, s * 256:(s + 1) * 256], scalar=dsm[:, 12 + s:13 + s], in1=subw[:], op0=ALU.mult, op1=ALU.mult),
                                 reads=[f"dtile{s}", "ars", "subw"], writes=[f"atb{s}"])
                        return f

                    def e_h():
                        arot["a"] += 1
                        ai = arot["a"] % 2
                        pt, pk = next_tr()

                        def fn(e):
                            ins = None
                            for j in range(8):
                                ins = e.transpose(out=pt[:, j * 128:(j + 1) * 128], in_=atb[:, j * 128:(j + 1) * 128], identity=identb[:])
                            return ins
                        P.op("pe", fn, reads=[f"atb{s}" for s in range(4)] + ["identb"], writes=[pk])

                        def e_i():
                            P.op("act", lambda e: e.activation(out=ast[ai][:].rearrange("p e (s t) -> p s e t", s=4), in_=pt[:, 0:1024].rearrange("p (s e t) -> p s e t", s=4, e=2), func=AF.Copy),
                                 reads=[pk], writes=[f"ast{ai}"])
                            dstap = mixT_s.ap()[h * 256:(h + 1) * 256, qb * 512:(qb + 1) * 512].rearrange("(e p) t -> p e t", p=128)
                            P.op("sp", lambda e: e.dma_start(out=dstap, in_=ast[ai][:]), reads=[f"ast{ai}"], pwrites=["mixT_s"], dsem=f"ast{ai}")
                        deferred.append((i + 24, e_i))
                    steps = [(2, e_a), (3, e_b(0)), (4, e_b(1)), (5, e_b(2)), (6, e_b(3)), (8, e_c(0)), (9, e_c(1)), (10, e_c(2)), (11, e_c(3)),
                             (13, e_d), (15, e_e), (17, e_f), (18, e_g(0)), (19, e_g(1)), (20, e_g(2)), (21, e_g(3)), (23, e_h)]
                    for off, f in steps:
                        deferred.append((i + off, f))

                deferred = []
                SK = 2
                load_head(0)
                load_head(1)
                nt = len(tiles)
                for i in range(nt + SK):
                    if i < nt:
                        front(i)
                    if i >= SK:
                        back(i - SK)
                        while deferred and deferred[0][0] <= i - SK:
                            deferred.pop(0)[1]()
                        h_done = tiles[i - SK][0]
                        if (i - SK == nt - 1 or tiles[i - SK + 1][0] != h_done) and h_done + 2 < NH:
                            load_head(h_done + 2)
                while deferred:
                    deferred.pop(0)[1]()
                P.emit()

            trs[:] = [(psT0, "psT0"), (psT1, "psT1")]
            with contextlib.ExitStack() as b1:
                sb = lambda name, shape, dt: b1.enter_context(nc.sbuf_tensor(name, shape, dt))
                mixT = sb("mixT", [128, 32, 1024], BF16)
                x1 = sb("x1", [128, 8, D], F32)
                wo = [sb(f"wo{i}", [128, 32, 128], BF16) for i in range(2)]
                nffn = sb("nffn", [128, D], F32)
                junk2 = sb("junk2", [128, D], BF16)
                hb2 = sb("hb2", [128, D], BF16)
                sm3 = sb("sm3", [128, 4], F32)
                h2st = [sb(f"h2st{i}", [128, 16, 128], BF16) for i in range(2)]
                P.op("sp", lambda e: e.dma_start(out=nffn[:], in_=pbc(nffn_d, D)), writes=["nffn"], dsem="c1")
                for i in range(4):
                    P.op("sp", lambda e, i=i: e.dma_start(out=mixT[:, i * 8:(i + 1) * 8, :], in_=mixT_s.ap()[i * 1024:(i + 1) * 1024, :].rearrange("(c p) t -> p c t", p=128)),
                         reads=["mixT_s"], pwrites=["mixT"], dsem="mixT")
                for tt in range(8):
                    P.op("act", lambda e, tt=tt: e.dma_start(out=x1[:, tt, :], in_=xctx.ap()[3072 + tt * 128:3072 + (tt + 1) * 128, :]), pwrites=["x1all"], dsem="x1ld")
                for nb in range(16):
                    b = nb % 2
                    src = w_out.ap().rearrange("(kc p) n -> p kc n", p=128)[:, :, nb * 128:(nb + 1) * 128]
                    P.op("pool", lambda e, b=b, src=src: e.dma_start(out=wo[b][:], in_=src), writes=[f"wo{b}"], dsem=f"wo{b}")
                    for tt in range(8):
                        acc, ak = next_acc()

                        def fn(e, acc=acc, tt=tt, b=b):
                            ins = None
                            for kc in range(32):
                                ins = e.matmul(acc[:, 0:128], lhsT=mixT[:, kc, tt * 128:(tt + 1) * 128], rhs=wo[b][:, kc, :], start=(kc == 0), stop=(kc == 31))
                            return ins
                        P.op("pe", fn, reads=["mixT", f"wo{b}"], writes=[ak])
                        P.op("dve", lambda e, acc=acc, tt=tt, nb=nb: e.tensor_tensor(out=x1[:, tt, nb * 128:(nb + 1) * 128], in0=x1[:, tt, nb * 128:(nb + 1) * 128], in1=acc[:, 0:128], op=ALU.add),
                             reads=[ak, "x1all", f"x1_{tt}"], writes=[f"x1_{tt}"])
                for tt in range(8):
                    P.op("sp", lambda e, tt=tt: e.dma_start(out=x1_s.ap()[tt * 128:(tt + 1) * 128, :], in_=x1[:, tt, :]), reads=[f"x1_{tt}"], pwrites=["x1_s"], dsem="x1st")
                    rms_rstd(x1[:, tt, :], f"x1_{tt}", D, 1e-6, junk2[:], sm3[:, 0:1], sm3[:, 1:2], "n3")
                    P.op("dve", lambda e, tt=tt: e.scalar_tensor_tensor(out=hb2[:], in0=x1[:, tt, :], scalar=sm3[:, 1:2], in1=nffn[:], op0=ALU.mult, op1=ALU.mult),
                         reads=[f"x1_{tt}", "n3rs", "nffn"], writes=["hb2"])
                    hi = tt % 2

                    def dst(g0, n, pt, pk, hi=hi):
                        P.op("act", lambda e: e.activation(out=h2st[hi][:, g0:g0 + n, :], in_=pt[:, 0:n * 128].rearrange("p (a b) -> p a b", a=n), func=AF.Copy),
                             reads=[pk], writes=[f"h2st{hi}_{g0}"])
                    transposes_to(hb2, "hb2", 16, dst, None)
                    dstap = h2T_s.ap()[:, tt * 128:(tt + 1) * 128].rearrange("(c p) t -> p c t", p=128)
                    P.op("sp", lambda e, hi=hi, dstap=dstap: e.dma_start(out=dstap, in_=h2st[hi][:]), reads=[f"h2st{hi}_0", f"h2st{hi}_8"], pwrites=["h2T_s"], dsem=f"h2st{hi}")
                    P.reads_since.setdefault(f"h2st{hi}_0", []).append(("d_" + f"h2st{hi}", P.count["d_" + f"h2st{hi}"]))
                    P.reads_since.setdefault(f"h2st{hi}_8", []).append(("d_" + f"h2st{hi}", P.count["d_" + f"h2st{hi}"]))
                P.emit()

            with contextlib.ExitStack() as b0:
                sbp = lambda name, shape, dt: b0.enter_context(nc.sbuf_tensor(name, shape, dt))
                aT = sbp("aT", [128, 44, 1024], BF16)
                wd = [sbp(f"wd{i}", [128, 44, 128], BF16) for i in range(1)]
                with contextlib.ExitStack() as b2:
                    sb = lambda name, shape, dt: b2.enter_context(nc.sbuf_tensor(name, shape, dt))
                    h2T = sb("h2T", [128, 16, 1024], BF16)
                    wg = [sb(f"wg{i}", [128, 16, 512], BF16) for i in range(2)]
                    wu = [sb(f"wu{i}", [128, 16, 512], BF16) for i in range(2)]
                    sg = [sb(f"sg{i}", [128, 512], F32) for i in range(2)]
                    for i in range(2):
                        P.op("sp", lambda e, i=i: e.dma_start(out=h2T[:, i * 8:(i + 1) * 8, :], in_=h2T_s.ap()[i * 1024:(i + 1) * 1024, :].rearrange("(c p) t -> p c t", p=128)),
                             reads=["h2T_s"], pwrites=["h2T"], dsem="h2T")
                    grot = 0
                    for fg in range(11):
                        b = fg % 2
                        srcg = w_gate.ap().rearrange("(kc p) n -> p kc n", p=128)[:, :, fg * 512:(fg + 1) * 512]
                        srcu = w_up.ap().rearrange("(kc p) n -> p kc n", p=128)[:, :, fg * 512:(fg + 1) * 512]
                        if fg == 0:
                            for j4 in range(4):
                                P.op("pool", lambda e, b=b, srcg=srcg, j4=j4: e.dma_start(out=wg[b][:, :, j4 * 128:(j4 + 1) * 128], in_=srcg[:, :, j4 * 128:(j4 + 1) * 128]), writes=[f"wg{b}_{j4}"], dsem=f"wg{b}_{j4}")
                                P.op("pool", lambda e, b=b, srcu=srcu, j4=j4: e.dma_start(out=wu[b][:, :, j4 * 128:(j4 + 1) * 128], in_=srcu[:, :, j4 * 128:(j4 + 1) * 128]), writes=[f"wu{b}_{j4}"], dsem=f"wu{b}_{j4}")
                        else:
                            P.op("pool", lambda e, b=b, srcg=srcg: e.dma_start(out=wg[b][:], in_=srcg), writes=[f"wg{b}"], dsem=f"wg{b}")
                            P.op("pool", lambda e, b=b, srcu=srcu: e.dma_start(out=wu[b][:], in_=srcu), writes=[f"wu{b}"], dsem=f"wu{b}")
                        for j in range(4):
                            fc = fg * 4 + j
                            for half in range(2):
                                hs = slice(half * 512, (half + 1) * 512)
                                gacc, gk = (psC, "psC") if half == 0 else (psD, "psD")
                                uacc, uk = (psM, "psM") if half == 0 else (psY, "psY")

                                def fn(e, gacc=gacc, b=b, j=j, hs=hs):
                                    ins = None
                                    for kc in range(16):
                                        ins = e.matmul(gacc[:], lhsT=wg[b][:, kc, j * 128:(j + 1) * 128], rhs=h2T[:, kc, hs], start=(kc == 0), stop=(kc == 15))
                                    return ins
                                P.op("pe", fn, reads=[f"wg{b}", "h2T"] + ([f"wg{b}_{j}"] if fg == 0 else []), writes=[gk])

                                def fn(e, uacc=uacc, b=b, j=j, hs=hs):
                                    ins = None
                                    for kc in range(16):
                                        ins = e.matmul(uacc[:], lhsT=wu[b][:, kc, j * 128:(j + 1) * 128], rhs=h2T[:, kc, hs], start=(kc == 0), stop=(kc == 15))
                                    return ins
                                P.op("pe", fn, reads=[f"wu{b}", "h2T"] + ([f"wu{b}_{j}"] if fg == 0 else []), writes=[uk])
                                grot += 1
                                gi = grot % 2
                                P.op("act", lambda e, gacc=gacc, gi=gi: e.activation(out=sg[gi][:], in_=gacc[:], func=AF.Silu), reads=[gk], writes=[f"sg{gi}"])
                                P.op("dve", lambda e, uacc=uacc, gi=gi, fc=fc, hs=hs: e.tensor_tensor(out=aT[:, fc, hs], in0=sg[gi][:], in1=uacc[:], op=ALU.mult), reads=[f"sg{gi}", uk], writes=["aT"])
                    for nb in range(1):
                        src = w_down.ap().rearrange("(kc p) n -> p kc n", p=128)[:, :, nb * 128:(nb + 1) * 128]
                        P.op("pool", lambda e, nb=nb, src=src: e.dma_start(out=wd[nb][:], in_=src), writes=[f"wd{nb}"], dsem=f"wd{nb}")
                    P.emit()
                with contextlib.ExitStack() as b3:
                    sb = lambda name, shape, dt: b3.enter_context(nc.sbuf_tensor(name, shape, dt))
                    wd.append(sb("wd1", [128, 44, 128], BF16))
                    wd.append(sb("wd2", [128, 44, 128], BF16))
                    ys = sb("ys", [128, 8, D], F32)
                    nfin = sb("nfin", [128, D], F32)
                    junk3 = sb("junk3", [128, D], BF16)
                    sm4 = sb("sm4", [128, 4], F32)
                    P.op("sp", lambda e: e.dma_start(out=nfin[:], in_=pbc(nfin_d, D)), writes=["nfin"], dsem="c1")
                    for tt in range(8):
                        P.op("sp", lambda e, tt=tt: e.dma_start(out=ys[:, tt, :], in_=x1_s.ap()[tt * 128:(tt + 1) * 128, :]), reads=["x1_s"], pwrites=["ysall"], dsem="ysld")
                    for nb in range(16):
                        b = nb % 3
                        src = w_down.ap().rearrange("(kc p) n -> p kc n", p=128)[:, :, nb * 128:(nb + 1) * 128]
                        if nb >= 1:
                            P.op("pool", lambda e, b=b, src=src: e.dma_start(out=wd[b][:], in_=src), writes=[f"wd{b}"], dsem=f"wd{b}")
                        for tt in range(8):
                            acc, ak = next_acc()

                            def fn(e, acc=acc, tt=tt, b=b):
                                ins = None
                                for kc in range(44):
                                    ins = e.matmul(acc[:, 0:128], lhsT=aT[:, kc, tt * 128:(tt + 1) * 128], rhs=wd[b][:, kc, :], start=(kc == 0), stop=(kc == 43))
                                return ins
                            P.op("pe", fn, reads=["aT", f"wd{b}"], writes=[ak])
                            P.op("dve", lambda e, acc=acc, tt=tt, nb=nb: e.tensor_tensor(out=ys[:, tt, nb * 128:(nb + 1) * 128], in0=ys[:, tt, nb * 128:(nb + 1) * 128], in1=acc[:, 0:128], op=ALU.add),
                                 reads=[ak, "ysall", f"ys{tt}"], writes=[f"ys{tt}"])
                    for tt in range(8):
                        rms_rstd(ys[:, tt, :], f"ys{tt}", D, 1e-6, junk3[:], sm4[:, 0:1], sm4[:, 1:2], "n4")
                        P.op("dve", lambda e, tt=tt: e.scalar_tensor_tensor(out=ys[:, tt, :], in0=ys[:, tt, :], scalar=sm4[:, 1:2], in1=nfin[:], op0=ALU.mult, op1=ALU.mult),
                             reads=[f"ys{tt}", "n4rs", "nfin"], writes=[f"ys{tt}"])
                        P.op("sp", lambda e, tt=tt: e.dma_start(out=out_d.ap()[tt * 128:(tt + 1) * 128, :], in_=ys[:, tt, :]), reads=[f"ys{tt}"], pwrites=["out"], dsem="outst")
                    P.final_wait("sp")
                    P.emit()
    return nc


def _consts():
    k = np.arange(128)
    ident = np.eye(128, dtype=np.float32)
    tri = (k[:, None] <= k[None, :]).astype(np.float32)
    U = (k[:, None] > k[None, :]).astype(np.float32)
    ones = np.ones((128, 128), np.float32)
    cst = np.concatenate([ident, tri, U, ones], axis=1)
    start = 2.0 ** (-8.0 / NH)
    slopes = np.array([start ** (i + 1) for i in range(NH)], dtype=np.float64)
    kl = np.arange(128)[:, None].astype(np.float64)
    ql = np.arange(512)[None, :].astype(np.float64)
    ab = np.zeros((NH, 128, 5, 512), np.float32)
    for h in range(NH):
        gen = slopes[h] * (kl - ql) / SCALE
        ab[h, :, 0, :] = gen
        for jd in range(4):
            masked = ql < (128 * jd + kl)
            ab[h, :, 1 + jd, :] = np.where(masked, NEG / SCALE, gen)
    return cst, ab.reshape(NH, 128, 5 * 512), slopes


_CACHE = {}


def kernel(x, norm_mix_w, w_in, lambda_q1, lambda_k1, lambda_q2, lambda_k2, subln_w,
           conv_w, conv_b, dt_bias, a_log, d_skip, ssm_norm_w, w_out,
           norm_ffn_w, w_gate, w_up, w_down, norm_final_w):
    f = lambda a: np.ascontiguousarray(np.asarray(a, dtype=np.float32))
    x = f(x)
    if "nc" not in _CACHE:
        _CACHE["nc"] = build_program()
    nc = _CACHE["nc"]
    cst, abias, slopes = _consts()
    cw = f(conv_w)[0]
    cwT = np.ascontiguousarray(cw.reshape(4, 24, 128).transpose(2, 1, 0)).reshape(128, 96)
    cbT = np.ascontiguousarray(f(conv_b)[0].reshape(24, 128).T)
    lam4 = np.stack([f(lambda_q1)[0], f(lambda_k1)[0], f(lambda_q2)[0], f(lambda_k2)[0]], axis=0)
    shared = {
        "w_in": f(w_in)[0], "w_out": f(w_out)[0], "w_gate": f(w_gate)[0], "w_up": f(w_up)[0], "w_down": f(w_down)[0],
        "nmixT": np.ascontiguousarray(f(norm_mix_w).reshape(16, 128).T), "norm_ffn_w": f(norm_ffn_w).reshape(1, D), "norm_final_w": f(norm_final_w).reshape(1, D),
        "lam4": np.ascontiguousarray(lam4), "subln_w": f(subln_w).reshape(1, 256), "cwT": cwT, "cbT": cbT,
        "dt_bias": f(dt_bias).reshape(1, 32), "a_log": f(a_log).reshape(1, 32), "d_skip": f(d_skip).reshape(1, 32),
        "snwT": np.ascontiguousarray(f(ssm_norm_w).reshape(16, 128).T), "cst": cst, "abias": abias,
    }
    in_maps = []
    for c in range(8):
        b, j = c // 4, c % 4
        pad = (3 - j) * 1024
        xc = np.zeros((SEQ, D), np.float32)
        xc[pad:] = x[b, 0:(j + 1) * 1024]
        tok = np.arange(SEQ).reshape(32, 128).T
        vmask = (tok >= pad).astype(np.float32)
        kb = np.zeros((128, 512), np.float32)
        for h in range(NH):
            for qb in range(2):
                for kbi in range(32):
                    v = slopes[h] * (kbi * 128 - (3072 + qb * 512)) + (NEG if kbi * 128 < pad else 0.0)
                    kb[:, h * 64 + qb * 32 + kbi] = v
        m = dict(shared)
        m.update({"xctx": xc, "vmask": np.ascontiguousarray(vmask), "kbias": kb})
        in_maps.append(m)
    res = run_bass_kernel_spmd(nc, in_maps, core_ids=list(range(8)))
    out = np.zeros((2, SEQ, D), np.float32)
    for c in range(8):
        b, j = c // 4, c % 4
        out[b, j * 1024:(j + 1) * 1024] = res.results[c]["out"]
    return out
```

```python
import contextlib
import math
import numpy as np
import concourse.bass as bass
import concourse.mybir as mybir
from concourse.bass_utils import run_bass_kernel_spmd

F32 = mybir.dt.float32
BF16 = mybir.dt.bfloat16
AF = mybir.ActivationFunctionType
ALU = mybir.AluOpType

D = 2048
SEQ = 4096
NH = 8
INW = 11296
DFF = 5632
OFF_Q, OFF_K, OFF_V, OFF_Z, OFF_XBC, OFF_DT = 0, 2048, 4096, 6144, 8192, 11264
SCALE = 128 ** -0.5
NEG = -30000.0
SAME_ENG_SYNC = True


class Prog:
    ENGS = ("pe", "act", "dve", "pool", "sp")

    def __init__(self, nc, stack):
        self.nc = nc
        self.stack = stack
        self.ops = {e: [] for e in self.ENGS}
        self.sems = {}
        self.count = {}
        self.last_write = {}
        self.reads_since = {}
        self.seen = {e: {} for e in self.ENGS}
        self.emitted = {e: 0 for e in self.ENGS}
        for e in self.ENGS:
            self._sem("eng_" + e)

    def _sem(self, name):
        if name not in self.sems:
            self.sems[name] = self.stack.enter_context(self.nc.semaphore(name))
            self.count[name] = 0
        return self.sems[name]

    def op(self, eng, fn, reads=(), writes=(), dsem=None, pwrites=()):
        if dsem is not None:
            sname = "d_" + dsem
            self._sem(sname)
            step = 16
        else:
            sname = "eng_" + eng
            step = 1
        self.count[sname] += step
        tok = (sname, self.count[sname])
        deps = {}

        def add(t):
            if t is None:
                return
            s, v = t
            if dsem is None and s == "eng_" + eng and (eng == "pe" or not SAME_ENG_SYNC):
                return
            if deps.get(s, 0) < v:
                deps[s] = v
        for k in list(reads) + list(writes):
            for t in self.last_write.get(k, {}).items():
                add(t)
        for k in list(writes) + list(pwrites):
            for t in self.reads_since.get(k, ()):
                add(t)
        waits = []
        seen = self.seen[eng]
        for s, v in deps.items():
            if seen.get(s, 0) < v:
                seen[s] = v
                waits.append((s, v))
        for k in writes:
            self.last_write[k] = {tok[0]: tok[1]}
            self.reads_since[k] = []
        for k in pwrites:
            self.last_write.setdefault(k, {})[tok[0]] = tok[1]
            self.reads_since[k] = []
        for k in reads:
            self.reads_since.setdefault(k, []).append(tok)
        self.ops[eng].append((waits, fn, sname, step))
        return tok

    def final_wait(self, eng="sp"):
        waits = [(s, v) for s, v in self.count.items() if v > 0 and s != "eng_" + eng]
        self.ops[eng].append((waits, None, None, 0))

    def emit(self):
        nc = self.nc
        with nc.Block() as block:
            for e, deco in (("pe", block.tensor), ("act", block.scalar), ("dve", block.vector),
                            ("pool", block.gpsimd), ("sp", block.sync)):
                ops = self.ops[e][self.emitted[e]:]
                self.emitted[e] = len(self.ops[e])

                def body(engobj, ops=ops):
                    for waits, fn, sname, step in ops:
                        for s, v in waits:
                            engobj.wait_ge(self.sems[s], v)
                        if fn is not None:
                            ins = fn(engobj)
                            ins.then_inc(self.sems[sname], step)
                deco(body)


def fbc(ap, n):
    return bass.AP(ap.tensor, ap.offset, [list(ap.ap[0]), list(ap.ap[1]), [0, n]])


def mbc(ap, m):
    return bass.AP(ap.tensor, ap.offset, [list(ap.ap[0]), [0, m], list(ap.ap[1])])


def pbc(t, n, off=0):
    return bass.AP(t, off, [[0, 128], [1, n]])


def build_program():
    nc = bass.Bass("TRN2", target_bir_lowering=False)
    din = lambda name, shape, dt=F32: nc.dram_tensor(name, shape, dt, kind="ExternalInput")
    xctx = din("xctx", [SEQ, D])
    vmask_d = din("vmask", [128, 32])
    kbias_d = din("kbias", [128, 512])
    w_in = din("w_in", [D, INW])
    w_out = din("w_out", [4096, D])
    w_gate = din("w_gate", [D, DFF])
    w_up = din("w_up", [D, DFF])
    w_down = din("w_down", [DFF, D])
    nmixT_d = din("nmixT", [128, 16])
    nffn_d = din("norm_ffn_w", [1, D])
    nfin_d = din("norm_final_w", [1, D])
    lam_d = din("lam4", [4, 128])
    subw_d = din("subln_w", [1, 256])
    cw_d = din("cwT", [128, 24 * 4])
    cb_d = din("cbT", [128, 24])
    dtb_d = din("dt_bias", [1, 32])
    alog_d = din("a_log", [1, 32])
    dsk_d = din("d_skip", [1, 32])
    snwT_d = din("snwT", [128, 16])
    cst_d = din("cst", [128, 4 * 128])
    abias_d = din("abias", [NH, 128, 5 * 512])
    out_d = nc.dram_tensor("out", [1024, D], F32, kind="ExternalOutput")
    dint = lambda name, shape, dt: nc.dram_tensor(name, shape, dt, kind="Internal")
    kT_s = dint("kT_s", [2048, SEQ], BF16)
    v_s = dint("v_s", [SEQ, 2048], BF16)
    qT_s = dint("qT_s", [2048, 1024], BF16)
    mixT_s = dint("mixT_s", [4096, 1024], BF16)
    x1_s = dint("x1_s", [1024, D], F32)
    h2T_s = dint("h2T_s", [2048, 1024], BF16)

    with contextlib.ExitStack() as st:
        P = Prog(nc, st)
        psum = lambda name, shape, dt: st.enter_context(nc.psum_tensor(name, shape, dt))
        psA = psum("psA", [128, 512], F32)
        psB = psum("psB", [128, 512], F32)
        psC = psum("psC", [128, 512], F32)
        psD = psum("psD", [128, 512], F32)
        psM = psum("psM", [128, 512], F32)
        psY = psum("psY", [128, 512], F32)
        psT0 = psum("psT0", [128, 1024], BF16)
        psT1 = psum("psT1", [128, 1024], BF16)
        rot = {"i": 0}
        accs = [(psA, "psA"), (psB, "psB")]

        def next_acc():
            rot["i"] += 1
            return accs[rot["i"] % 2]
        trot = {"i": 0}
        trs = [(psT0, "psT0"), (psT1, "psT1")]

        def next_tr():
            trot["i"] += 1
            return trs[trot["i"] % len(trs)]

        with contextlib.ExitStack() as cs:
            sb = lambda name, shape, dt: cs.enter_context(nc.sbuf_tensor(name, shape, dt))
            cstf = sb("cstf", [128, 512], F32)
            identb = sb("identb", [128, 128], BF16)
            onescol = sb("onescol", [128, 1], F32)
            P.op("sp", lambda e: e.dma_start(out=cstf[:], in_=cst_d.ap()), writes=["cstf"], dsem="c0")
            P.op("dve", lambda e: e.tensor_copy(out=identb[:], in_=cstf[:, 0:128]), reads=["cstf"], writes=["identb"])
            P.op("dve", lambda e: e.memset(onescol[:], 1.0), writes=["onescol"])
            tri = cstf[:, 128:256]
            U = cstf[:, 256:384]
            ones = cstf[:, 384:512]

            def rms_rstd(xt_ap, xkey, n, eps, junk, ss, rs, tag, jkey=None):
                P.op("act", lambda e: e.activation(out=junk, in_=xt_ap, func=AF.Square, accum_out=ss),
                     reads=[xkey], writes=[tag + "ss", jkey or (tag + "junk")])
                P.op("dve", lambda e: e.tensor_scalar(out=rs, in0=ss, scalar1=1.0 / n, scalar2=eps, op0=ALU.mult, op1=ALU.add),
                     reads=[tag + "ss"], writes=[tag + "rs"])
                P.op("act", lambda e: e.activation(out=rs, in_=rs, func=AF.Sqrt), reads=[tag + "rs"], writes=[tag + "rs"])
                P.op("dve", lambda e: e.reciprocal(out=rs, in_=rs), reads=[tag + "rs"], writes=[tag + "rs"])

            def transposes_to(src_tile, src_key, nchunk, dst_fn, dst_keys, evac_eng="act"):
                for g0 in range(0, nchunk, 8):
                    n = min(8, nchunk - g0)
                    pt, pk = next_tr()

                    def fn(e, g0=g0, n=n, pt=pt):
                        ins = None
                        for i in range(n):
                            ins = e.transpose(out=pt[:, i * 128:(i + 1) * 128], in_=src_tile[:, (g0 + i) * 128:(g0 + i + 1) * 128], identity=identb[:])
                        return ins
                    P.op("pe", fn, reads=[src_key, "identb"], writes=[pk])
                    dst_fn(g0, n, pt, pk)

            def interleave(threads):
                threads = [t for t in threads if t is not None]
                while threads:
                    for t in list(threads):
                        try:
                            next(t)
                        except StopIteration:
                            threads.remove(t)

            with contextlib.ExitStack() as a1:
                sb = lambda name, shape, dt: a1.enter_context(nc.sbuf_tensor(name, shape, dt))
                nmixT = sb("nmixT_sb", [128, 16], F32)
                snwT = sb("snwT_sb", [128, 16], F32)
                xt = [sb(f"xt{i}", [128, D], F32) for i in range(2)]
                hb = [sb(f"hb{i}", [128, D], BF16) for i in range(2)]
                hT = [sb(f"hT{i}", [128, 16, 512], BF16) for i in range(2)]
                NW = 3
                wt = [sb(f"wt{i}", [128, 16, 512], BF16) for i in range(NW)]
                small = sb("small", [128, 64], F32)
                vmask = sb("vmask_sb", [128, 32], F32)
                cw = sb("cw", [128, 96], F32)
                cb = sb("cb", [128, 24], F32)
                halo = sb("halo", [128, 24, 3], F32)
                raw = [sb(f"raw{i}", [128, 515], F32) for i in range(2)]
                cacc = [sb(f"cacc{i}", [128, 512], F32) for i in range(2)]
                cfm = [sb(f"cfm{i}", [128, 512], BF16) for i in range(4)]
                Xtok = sb("Xtok", [128, 4, 2048], BF16)
                Btok = sb("Btok", [128, 4, 512], BF16)
                BT = sb("BT", [128, 4, 512], BF16)
                CT = sb("CT", [128, 4, 512], BF16)
                Hs = sb("Hs", [128, 2048], F32)
                Hb = sb("Hb", [128, 2048], BF16)
                dtb = sb("dtb", [128, 32], F32)
                Abc = sb("Abc", [128, 32], F32)
                dsk = sb("dsk", [128, 32], F32)
                dtt = sb("dtt", [128, 4, 32], F32)
                adt = sb("adt", [128, 4, 32], F32)
                sm2 = [sb(f"sm2_{i}", [128, 6, 32], F32) for i in range(2)]
                Xd = sb("Xd", [128, 2048], BF16)
                kst = [sb(f"kst{i}", [128, 512], BF16) for i in range(2)]
                vst = [sb(f"vst{i}", [128, 512], BF16) for i in range(2)]
                zs = sb("zs", [128, 4, 2048], BF16)
                rseg = sb("rseg", [128, 1024], F32)
                Lt = sb("Lt", [128, 8, 128], F32)
                CBm = sb("CBm", [128, 128], F32)
                Mt = sb("Mt", [128, 8, 128], BF16)
                yb = sb("yb", [128, 512], F32)
                yo = sb("yo", [128, 512], BF16)
                mst = [sb(f"mst{i}", [128, 4, 128], BF16) for i in range(2)]
                Htmp = sb("Htmp", [128, 512], F32)

                ld = lambda eng, out, in_, key, dsem: P.op(eng, lambda e: e.dma_start(out=out, in_=in_), writes=[key], dsem=dsem)
                ld("sp", nmixT[:], nmixT_d.ap(), "nmixT", "c1")
                ld("sp", vmask[:], vmask_d.ap(), "vmask", "c2")
                ld("sp", cw[:], cw_d.ap(), "cw", "c3")
                ld("sp", cb[:], cb_d.ap(), "cb", "c4")
                ld("sp", dtb[:], pbc(dtb_d, 32), "dtb", "c5")
                ld("sp", Abc[:], pbc(alog_d, 32), "Abc", "c6")
                ld("sp", dsk[:], pbc(dsk_d, 32), "dsk", "c7")
                ld("sp", snwT[:], snwT_d.ap(), "snwT", "c8")
                P.op("act", lambda e: e.activation(out=Abc[:], in_=Abc[:], func=AF.Exp), reads=["Abc"], writes=["Abc"])
                P.op("dve", lambda e: e.tensor_scalar(out=Abc[:], in0=Abc[:], scalar1=-1.0, scalar2=None, op0=ALU.mult), reads=["Abc"], writes=["Abc"])
                P.op("dve", lambda e: e.memset(halo[:], 0.0), writes=["halo"])
                P.op("dve", lambda e: e.memset(Hs[:], 0.0), writes=[f"Hs{g}" for g in range(4)])
                P.op("dve", lambda e: e.memset(Hb[:], 0.0), writes=[f"Hb{g}" for g in range(4)])

                wrot = {"i": 0}

                def load_w(c0, ncols):
                    wrot["i"] += 1
                    b = wrot["i"] % NW
                    src = w_in.ap().rearrange("(kc p) n -> p kc n", p=128)[:, :, c0:c0 + ncols]
                    P.op("pool", lambda e: e.dma_start(out=wt[b][:, :, 0:ncols], in_=src), writes=[f"wt{b}"], dsem=f"wt{b}")
                    return wt[b], f"wt{b}"

                def fm_group(w, wk, j, hp):
                    acc, ak = next_acc()

                    def fn(e):
                        ins = None
                        for kc in range(16):
                            ins = e.matmul(acc[:], lhsT=w[:, kc, j * 128:(j + 1) * 128], rhs=hT[hp][:, kc, :], start=(kc == 0), stop=(kc == 15))
                        return ins
                    P.op("pe", fn, reads=[wk, f"hT{hp}"], writes=[ak])
                    return acc, ak

                def tm_group(w, wk, tt, ncols, hp):
                    acc, ak = next_acc()

                    def fn(e):
                        ins = None
                        for kc in range(16):
                            ins = e.matmul(acc[:, 0:ncols], lhsT=hT[hp][:, kc, tt * 128:(tt + 1) * 128], rhs=w[:, kc, 0:ncols], start=(kc == 0), stop=(kc == 15))
                        return ins
                    P.op("pe", fn, reads=[wk, f"hT{hp}"], writes=[ak])
                    return acc, ak

                strot = {"k": 0, "v": 0, "r": 0, "m": 0, "c": 0, "s": 0}

                def th_norm(blk):
                    hp = blk % 2

                    def load(tt):
                        t = blk * 4 + tt
                        b = t % 2
                        P.op("sp", lambda e: e.dma_start(out=xt[b][:], in_=xctx.ap()[t * 128:(t + 1) * 128, :]), writes=[f"xt{b}"], dsem=f"xt{b}")

                    def stats(tt):
                        t = blk * 4 + tt
                        b = t % 2
                        rms_rstd(xt[b][:], f"xt{b}", D, 1e-6, hb[b][:], small[:, 0:1], small[:, 1:2], "n1", jkey=f"hb{b}")
                        P.op("dve", lambda e: e.tensor_scalar(out=hb[b][:], in0=xt[b][:], scalar1=small[:, 1:2], scalar2=None, op0=ALU.mult),
                             reads=[f"xt{b}", "n1rs"], writes=[f"hb{b}"])

                    def trans(tt):
                        t = blk * 4 + tt
                        b = t % 2

                        def dst(g0, n, pt, pk):
                            P.op("dve", lambda e: e.tensor_tensor(out=hT[hp][:, g0:g0 + n, tt * 128:(tt + 1) * 128],
                                                               in0=pt[:, 0:n * 128].rearrange("p (a b) -> p a b", a=n), in1=fbc(nmixT[:, g0:g0 + n], 128), op=ALU.mult),
                                 reads=[pk, "nmixT"], writes=[f"hT{hp}"])
                        transposes_to(hb[b], f"hb{b}", 16, dst, None)
                    load(0)
                    load(1)
                    yield
                    yield
                    for tt in range(4):
                        stats(tt)
                        if tt + 2 < 4:
                            load(tt + 2)
                        yield
                        yield
                        yield
                        if tt >= 1:
                            trans(tt - 1)
                            yield
                    yield
                    yield
                    trans(3)
                    yield

                def th_kvq(blk):
                    hp = blk % 2
                    own = blk >= 6
                    ob = blk - 6
                    for cg in range(4):
                        w, wk = load_w(OFF_K + cg * 512, 512)
                        for j in range(4):
                            acc, ak = fm_group(w, wk, j, hp)
                            strot["k"] += 1
                            sbi = strot["k"] % 2
                            P.op("act", lambda e, acc=acc, sbi=sbi: e.activation(out=kst[sbi][:], in_=acc[:], func=AF.Copy), reads=[ak], writes=[f"kst{sbi}"])
                            row = (cg * 4 + j) * 128
                            P.op("sp", lambda e, sbi=sbi, row=row: e.dma_start(out=kT_s.ap()[row:row + 128, blk * 512:(blk + 1) * 512], in_=kst[sbi][:]),
                                 reads=[f"kst{sbi}"], pwrites=["kT_s"], dsem=f"kst{sbi}")
                            yield
                    for cg in range(4):
                        w, wk = load_w(OFF_V + cg * 512, 512)
                        for tt in range(4):
                            acc, ak = tm_group(w, wk, tt, 512, hp)
                            strot["v"] += 1
                            sbi = strot["v"] % 2
                            P.op("dve", lambda e, acc=acc, sbi=sbi: e.tensor_copy(out=vst[sbi][:], in_=acc[:]), reads=[ak], writes=[f"vst{sbi}"])
                            r0 = (blk * 4 + tt) * 128
                            P.op("sp", lambda e, sbi=sbi, r0=r0, cg=cg: e.dma_start(out=v_s.ap()[r0:r0 + 128, cg * 512:(cg + 1) * 512], in_=vst[sbi][:]),
                                 reads=[f"vst{sbi}"], pwrites=["v_s"], dsem=f"vst{sbi}")
                            yield
                    if own:
                        for cg in range(4):
                            w, wk = load_w(OFF_Q + cg * 512, 512)
                            for j in range(4):
                                acc, ak = fm_group(w, wk, j, hp)
                                strot["k"] += 1
                                sbi = strot["k"] % 2
                                P.op("act", lambda e, acc=acc, sbi=sbi: e.activation(out=kst[sbi][:], in_=acc[:], func=AF.Copy), reads=[ak], writes=[f"kst{sbi}"])
                                row = (cg * 4 + j) * 128
                                P.op("sp", lambda e, sbi=sbi, row=row: e.dma_start(out=qT_s.ap()[row:row + 128, ob * 512:(ob + 1) * 512], in_=kst[sbi][:]),
                                     reads=[f"kst{sbi}"], pwrites=["qT_s"], dsem=f"kst{sbi}")
                                yield

                def xbc_dt_z(blk):
                    hp = blk % 2
                    own = blk >= 6
                    if own:
                        for cg in range(4):
                            w, wk = load_w(OFF_Z + cg * 512, 512)
                            for tt in range(4):
                                acc, ak = tm_group(w, wk, tt, 512, hp)
                                P.op("act", lambda e, acc=acc, tt=tt, cg=cg: e.activation(out=zs[:, tt, cg * 512:(cg + 1) * 512], in_=acc[:], func=AF.Silu),
                                     reads=[ak], writes=["zs"])
                    w, wk = load_w(OFF_DT, 32)
                    for tt in range(4):
                        t = blk * 4 + tt
                        acc, ak = tm_group(w, wk, tt, 32, hp)
                        P.op("dve", lambda e, acc=acc, tt=tt: e.tensor_tensor(out=dtt[:, tt, :], in0=acc[:, 0:32], in1=dtb[:], op=ALU.add), reads=[ak, "dtb"], writes=["dtt"])
                        P.op("act", lambda e, tt=tt: e.activation(out=dtt[:, tt, :], in_=dtt[:, tt, :], func=AF.Exp), reads=["dtt"], writes=["dtt"])
                        P.op("act", lambda e, tt=tt: e.activation(out=dtt[:, tt, :], in_=dtt[:, tt, :], func=AF.Ln, bias=onescol[:, 0:1], scale=1.0), reads=["dtt", "onescol"], writes=["dtt"])
                        P.op("dve", lambda e, tt=tt, t=t: e.tensor_scalar(out=dtt[:, tt, :], in0=dtt[:, tt, :], scalar1=vmask[:, t:t + 1], scalar2=None, op0=ALU.mult), reads=["dtt", "vmask"], writes=["dtt"])
                        P.op("dve", lambda e, tt=tt: e.tensor_tensor(out=adt[:, tt, :], in0=dtt[:, tt, :], in1=Abc[:], op=ALU.mult), reads=["dtt", "Abc"], writes=["adt"])
                    pending = []

                    def flush(upto):
                        while len(pending) > upto:
                            pending.pop(0)()
                    for cg in range(6 if blk >= 5 else 5):
                        w, wk = load_w(OFF_XBC + cg * 512, 512)
                        for j in range(4):
                            ci = cg * 4 + j
                            acc, ak = fm_group(w, wk, j, hp)
                            strot["r"] += 1
                            rb = strot["r"] % 2
                            P.op("act", lambda e, acc=acc, rb=rb: e.activation(out=raw[rb][:, 3:515], in_=acc[:], func=AF.Copy), reads=[ak], writes=[f"raw{rb}"])
                            P.op("act", lambda e, rb=rb, ci=ci: e.activation(out=raw[rb][:, 0:3], in_=halo[:, ci, :], func=AF.Copy), reads=["halo"], writes=[f"raw{rb}"])
                            P.op("dve", lambda e, rb=rb, ci=ci: e.tensor_scalar(out=cacc[rb][:], in0=raw[rb][:, 3:515], scalar1=cw[:, ci * 4 + 3:ci * 4 + 4], scalar2=cb[:, ci:ci + 1], op0=ALU.mult, op1=ALU.add),
                                 reads=[f"raw{rb}", "cw", "cb"], writes=[f"cacc{rb}"])
                            for k in range(3):
                                P.op("dve", lambda e, rb=rb, ci=ci, k=k: e.scalar_tensor_tensor(out=cacc[rb][:], in0=raw[rb][:, k:k + 512], scalar=cw[:, ci * 4 + k:ci * 4 + k + 1], in1=cacc[rb][:], op0=ALU.mult, op1=ALU.add),
                                     reads=[f"raw{rb}", "cw", f"cacc{rb}"], writes=[f"cacc{rb}"])
                            P.op("act", lambda e, rb=rb, ci=ci: e.activation(out=halo[:, ci, :], in_=raw[rb][:, 512:515], func=AF.Copy), reads=[f"raw{rb}"], writes=["halo"])
                            if ci < 16:
                                strot["c"] += 1
                                cf = strot["c"] % 4
                                P.op("act", lambda e, rb=rb, cf=cf: e.activation(out=cfm[cf][:], in_=cacc[rb][:], func=AF.Silu), reads=[f"cacc{rb}"], writes=[f"cfm{cf}"])

                                def later(ci=ci, cf=cf):
                                    def dst(g0, n, pt, pk):
                                        P.op("dve", lambda e: e.tensor_copy(out=Xtok[:, :, ci * 128:(ci + 1) * 128], in_=pt[:, 0:512].rearrange("p (a b) -> p a b", a=4)),
                                             reads=[pk], writes=["Xtok"])
                                    transposes_to(cfm[cf], f"cfm{cf}", 4, dst, None)
                                pending.append(later)
                            elif ci < 20:
                                g = ci - 16
                                P.op("act", lambda e, rb=rb, g=g: e.activation(out=BT[:, g, :], in_=cacc[rb][:], func=AF.Silu), reads=[f"cacc{rb}"], writes=[f"BT{g}"])

                                def later(g=g):
                                    def dst(g0, n, pt, pk):
                                        P.op("dve", lambda e: e.tensor_copy(out=Btok[:, :, g * 128:(g + 1) * 128], in_=pt[:, 0:512].rearrange("p (a b) -> p a b", a=4)),
                                             reads=[pk], writes=["Btok"])
                                    transposes_to(BT[:, g, :], f"BT{g}", 4, dst, None)
                                pending.append(later)
                            else:
                                g = ci - 20
                                P.op("act", lambda e, rb=rb, g=g: e.activation(out=CT[:, g, :], in_=cacc[rb][:], func=AF.Silu), reads=[f"cacc{rb}"], writes=[f"CT{g}"])
                            flush(2)
                    return pending

                def th_ssd(blk):
                    own = blk >= 6
                    ob = blk - 6
                    for tt in range(4):
                        ot = ob * 4 + tt
                        strot["s"] += 1
                        sm = sm2[strot["s"] % 2]
                        smk = f"sm{strot['s'] % 2}"

                        def fn(e, tt=tt):
                            e.matmul(psM[:, 0:32], lhsT=tri, rhs=adt[:, tt, :], start=True, stop=True)
                            return e.matmul(psM[:, 32:64], lhsT=ones, rhs=adt[:, tt, :], start=True, stop=True)
                        P.op("pe", fn, reads=["adt", "cstf"], writes=["psM"])
                        yield
                        dd, dsd, cd, ea = sm[:, 0, :], sm[:, 1, :], sm[:, 2, :], sm[:, 3, :]
                        P.op("dve", lambda e, sm=sm: e.tensor_copy(out=sm[:, 4, :], in_=psM[:, 0:32]), reads=["psM"], writes=[smk + "acol"])
                        P.op("dve", lambda e, sm=sm, dd=dd: e.tensor_tensor(out=dd, in0=psM[:, 32:64], in1=sm[:, 4, :], op=ALU.subtract), reads=["psM", smk + "acol"], writes=[smk + "dd"])
                        P.op("act", lambda e, dsd=dsd, dd=dd: e.activation(out=dsd, in_=dd, func=AF.Exp), reads=[smk + "dd"], writes=[smk + "dsd"])
                        P.op("act", lambda e, cd=cd: e.activation(out=cd, in_=psM[:, 32:64], func=AF.Exp), reads=["psM"], writes=[smk + "cd"])
                        if own:
                            P.op("act", lambda e, ea=ea: e.activation(out=ea, in_=psM[:, 0:32], func=AF.Exp), reads=["psM"], writes=[smk + "ea"])
                        P.op("dve", lambda e, tt=tt, dsd=dsd: e.tensor_tensor(out=dsd, in0=dsd, in1=dtt[:, tt, :], op=ALU.mult), reads=[smk + "dsd", "dtt"], writes=[smk + "dsd"])
                        P.op("dve", lambda e, tt=tt, dsd=dsd: e.tensor_tensor(out=Xd[:].rearrange("p (a b) -> p a b", a=32), in0=Xtok[:, tt, :].rearrange("p (a b) -> p a b", a=32), in1=fbc(dsd, 64), op=ALU.mult),
                             reads=["Xtok", smk + "dsd"], writes=["Xd"])
                        yield
                        for g in range(4):
                            gs = slice(g * 512, (g + 1) * 512)
                            y3 = lambda ap: ap.rearrange("p (a b) -> p a b", a=8)
                            if own:
                                P.op("dve", lambda e, tt=tt, g=g: e.tensor_tensor(out=rseg[:].rearrange("p (a b) -> p a b", a=8), in0=mbc(tri, 8), in1=fbc(adt[:, tt, g * 8:(g + 1) * 8], 128), op=ALU.mult),
                                     reads=["adt", "cstf"], writes=["rseg"])
                                P.op("pe", lambda e, tt=tt, g=g: e.matmul(psM[:, 128:256], lhsT=BT[:, g, tt * 128:(tt + 1) * 128], rhs=CT[:, g, tt * 128:(tt + 1) * 128], start=True, stop=True),
                                     reads=[f"BT{g}", f"CT{g}"], writes=["psM"])
                                P.op("dve", lambda e: e.tensor_tensor(out=CBm[:], in0=psM[:, 128:256], in1=tri, op=ALU.mult), reads=["psM", "cstf"], writes=["CBm"])
                                acc, ak = psD, "psD"
                                P.op("pe", lambda e, acc=acc, tt=tt, g=g, gs=gs: e.matmul(acc[:], lhsT=CT[:, g, tt * 128:(tt + 1) * 128], rhs=Hb[:, gs], start=True, stop=True),
                                     reads=[f"CT{g}", f"Hb{g}"], writes=[ak])
                                P.op("dve", lambda e, acc=acc, g=g, ea=ea: e.tensor_tensor(out=y3(yb[:]), in0=y3(acc[:]), in1=fbc(ea[:, g * 8:(g + 1) * 8], 64), op=ALU.mult), reads=[ak, smk + "ea"], writes=["yb"])
                                P.op("dve", lambda e, tt=tt, g=g, gs=gs: e.tensor_tensor(out=y3(Htmp[:]), in0=y3(Xtok[:, tt, gs]), in1=fbc(dsk[:, g * 8:(g + 1) * 8], 64), op=ALU.mult), reads=["Xtok", "dsk"], writes=["Htmp"])
                                yield

                                P.op("pe", lambda e: e.matmul(psC[:], lhsT=U, rhs=rseg[:, 0:512], start=True, stop=True), reads=["rseg", "cstf"], writes=["psC"])
                                P.op("act", lambda e: e.activation(out=Lt[:, 0:4, :], in_=psC[:].rearrange("p (a b) -> p a b", a=4), func=AF.Exp), reads=["psC"], writes=["Lt0"])
                                P.op("pe", lambda e: e.matmul(psC[:], lhsT=U, rhs=rseg[:, 512:1024], start=True, stop=True), reads=["rseg", "cstf"], writes=["psC"])
                                P.op("act", lambda e: e.activation(out=Lt[:, 4:8, :], in_=psC[:].rearrange("p (a b) -> p a b", a=4), func=AF.Exp), reads=["psC"], writes=["Lt1"])
                                P.op("dve", lambda e: e.tensor_tensor(out=Lt[:], in0=Lt[:], in1=mbc(CBm[:], 8), op=ALU.mult), reads=["Lt0", "Lt1", "CBm"], writes=["Lt0", "Lt1"])
                                P.op("dve", lambda e, tt=tt, g=g: e.tensor_tensor(out=Mt[:], in0=Lt[:], in1=fbc(dtt[:, tt, g * 8:(g + 1) * 8], 128), op=ALU.mult), reads=["Lt0", "Lt1", "dtt"], writes=["Mt"])
                                yield
                            acc2, ak2 = (psD, "psD") if (own or g % 2) else (psC, "psC")
                            P.op("pe", lambda e, acc2=acc2, tt=tt, g=g, gs=gs: e.matmul(acc2[:], lhsT=Btok[:, tt, g * 128:(g + 1) * 128], rhs=Xd[:, gs], start=True, stop=True),
                                 reads=["Btok", "Xd"], writes=[ak2])
                            P.op("dve", lambda e, g=g, gs=gs, cd=cd: e.tensor_tensor(out=y3(Hs[:, gs]), in0=y3(Hs[:, gs]), in1=fbc(cd[:, g * 8:(g + 1) * 8], 64), op=ALU.mult), reads=[f"Hs{g}", smk + "cd"], writes=[f"Hs{g}"])
                            P.op("dve", lambda e, acc2=acc2, gs=gs: e.tensor_tensor(out=Hs[:, gs], in0=Hs[:, gs], in1=acc2[:], op=ALU.add), reads=[f"Hs{g}", ak2], writes=[f"Hs{g}"])
                            P.op("act", lambda e, gs=gs: e.activation(out=Hb[:, gs], in_=Hs[:, gs], func=AF.Copy), reads=[f"Hs{g}"], writes=[f"Hb{g}"])
                            yield
                            if own:
                                def fn(e, tt=tt, g=g):
                                    ins = None
                                    for r in range(8):
                                        hh = g * 8 + r
                                        ins = e.matmul(psY[:, r * 64:(r + 1) * 64], lhsT=Mt[:, r, :], rhs=Xtok[:, tt, hh * 64:(hh + 1) * 64], start=True, stop=True)
                                    return ins
                                P.op("pe", fn, reads=["Mt", "Xtok"], writes=["psY"])
                                P.op("dve", lambda e: e.tensor_tensor(out=yb[:], in0=yb[:], in1=psY[:], op=ALU.add), reads=["yb", "psY"], writes=["yb"])
                                P.op("dve", lambda e: e.tensor_tensor(out=yb[:], in0=yb[:], in1=Htmp[:], op=ALU.add), reads=["yb", "Htmp"], writes=["yb"])
                                P.op("dve", lambda e, tt=tt, gs=gs: e.tensor_tensor(out=yb[:], in0=yb[:], in1=zs[:, tt, gs], op=ALU.mult), reads=["yb", "zs"], writes=["yb"])
                                rms_rstd(yb[:], "yb", 512, 1e-5, yo[:], small[:, 2:3], small[:, 3:4], "n2", jkey="yo")
                                P.op("dve", lambda e: e.tensor_scalar(out=yo[:], in0=yb[:], scalar1=small[:, 3:4], scalar2=None, op0=ALU.mult), reads=["yb", "n2rs"], writes=["yo"])
                                yield
                                strot["m"] += 1
                                mb = strot["m"] % 2

                                def dst(g0, n, pt, pk, mb=mb, g=g):
                                    P.op("dve", lambda e: e.tensor_tensor(out=mst[mb][:], in0=pt[:, 0:512].rearrange("p (a b) -> p a b", a=4), in1=fbc(snwT[:, g * 4:(g + 1) * 4], 128), op=ALU.mult),
                                         reads=[pk, "snwT"], writes=[f"mst{mb}"])
                                transposes_to(yo, "yo", 4, dst, None)
                                r0 = 2048 + g * 512
                                dstap = mixT_s.ap()[r0:r0 + 512, ot * 128:(ot + 1) * 128].rearrange("(a p) t -> p a t", p=128)
                                P.op("sp", lambda e, mb=mb, dstap=dstap: e.dma_start(out=dstap, in_=mst[mb][:]), reads=[f"mst{mb}"], pwrites=["mixT_s"], dsem=f"mst{mb}")
                                yield

                def th_ssd_own(blk):
                    ob = blk - 6
                    y3 = lambda ap: ap.rearrange("p (a b) -> p a b", a=8)
                    chunk_sm = {}

                    def chunk_ops(tt):
                        strot["s"] += 1
                        sm = sm2[strot["s"] % 2]
                        smk = f"sm{strot['s'] % 2}"
                        chunk_sm[tt] = (sm, smk)

                        def fn(e):
                            e.matmul(psM[:, 0:32], lhsT=tri, rhs=adt[:, tt, :], start=True, stop=True)
                            return e.matmul(psM[:, 32:64], lhsT=ones, rhs=adt[:, tt, :], start=True, stop=True)
                        P.op("pe", fn, reads=["adt", "cstf"], writes=["psM"])
                        dd, dsd, cd, ea = sm[:, 0, :], sm[:, 1, :], sm[:, 2, :], sm[:, 3, :]
                        P.op("dve", lambda e: e.tensor_copy(out=sm[:, 4, :], in_=psM[:, 0:32]), reads=["psM"], writes=[smk + "acol"])
                        P.op("dve", lambda e: e.tensor_tensor(out=dd, in0=psM[:, 32:64], in1=sm[:, 4, :], op=ALU.subtract), reads=["psM", smk + "acol"], writes=[smk + "dd"])
                        P.op("act", lambda e: e.activation(out=dsd, in_=dd, func=AF.Exp), reads=[smk + "dd"], writes=[smk + "dsd"])
                        P.op("act", lambda e: e.activation(out=cd, in_=psM[:, 32:64], func=AF.Exp), reads=["psM"], writes=[smk + "cd"])
                        P.op("act", lambda e: e.activation(out=ea, in_=psM[:, 0:32], func=AF.Exp), reads=["psM"], writes=[smk + "ea"])
                        P.op("dve", lambda e: e.tensor_tensor(out=dsd, in0=dsd, in1=dtt[:, tt, :], op=ALU.mult), reads=[smk + "dsd", "dtt"], writes=[smk + "dsd"])

                    def xd_op(tt):
                        sm, smk = chunk_sm[tt]
                        dsd = sm[:, 1, :]
                        P.op("dve", lambda e: e.tensor_tensor(out=Xd[:].rearrange("p (a b) -> p a b", a=32), in0=Xtok[:, tt, :].rearrange("p (a b) -> p a b", a=32), in1=fbc(dsd, 64), op=ALU.mult),
                             reads=["Xtok", smk + "dsd"], writes=["Xd"])

                    def head(tt, g):
                        P.op("dve", lambda e: e.tensor_tensor(out=rseg[:].rearrange("p (a b) -> p a b", a=8), in0=mbc(tri, 8), in1=fbc(adt[:, tt, g * 8:(g + 1) * 8], 128), op=ALU.mult),
                             reads=["adt", "cstf"], writes=["rseg"])
                        P.op("pe", lambda e: e.matmul(psM[:, 128:256], lhsT=BT[:, g, tt * 128:(tt + 1) * 128], rhs=CT[:, g, tt * 128:(tt + 1) * 128], start=True, stop=True),
                             reads=[f"BT{g}", f"CT{g}"], writes=["psM"])
                        P.op("dve", lambda e: e.tensor_tensor(out=CBm[:], in0=psM[:, 128:256], in1=tri, op=ALU.mult), reads=["psM", "cstf"], writes=["CBm"])
                        yield
                        P.op("pe", lambda e: e.matmul(psC[:], lhsT=U, rhs=rseg[:, 0:512], start=True, stop=True), reads=["rseg", "cstf"], writes=["psC"])
                        P.op("act", lambda e: e.activation(out=Lt[:, 0:4, :], in_=psC[:].rearrange("p (a b) -> p a b", a=4), func=AF.Exp), reads=["psC"], writes=["Lt0"])
                        acc, ak = next_acc()
                        P.op("pe", lambda e: e.matmul(acc[:], lhsT=U, rhs=rseg[:, 512:1024], start=True, stop=True), reads=["rseg", "cstf"], writes=[ak])
                        P.op("act", lambda e: e.activation(out=Lt[:, 4:8, :], in_=acc[:].rearrange("p (a b) -> p a b", a=4), func=AF.Exp), reads=[ak], writes=["Lt1"])
                        yield
                        P.op("dve", lambda e: e.tensor_tensor(out=Lt[:], in0=Lt[:], in1=mbc(CBm[:], 8), op=ALU.mult), reads=["Lt0", "Lt1", "CBm"], writes=["Lt0", "Lt1"])
                        P.op("dve", lambda e: e.tensor_tensor(out=Mt[:], in0=Lt[:], in1=fbc(dtt[:, tt, g * 8:(g + 1) * 8], 128), op=ALU.mult), reads=["Lt0", "Lt1", "dtt"], writes=["Mt"])
                        yield

                    def tail(tt, g):
                        sm, smk = chunk_sm[tt]
                        cd, ea = sm[:, 2, :], sm[:, 3, :]
                        gs = slice(g * 512, (g + 1) * 512)
                        ot = ob * 4 + tt
                        def fn(e):
                            ins = None
                            for r in range(8):
                                hh = g * 8 + r
                                ins = e.matmul(psY[:, r * 64:(r + 1) * 64], lhsT=Mt[:, r, :], rhs=Xtok[:, tt, hh * 64:(hh + 1) * 64], start=True, stop=True)
                            return ins
                        P.op("pe", fn, reads=["Mt", "Xtok"], writes=["psY"])
                        P.op("pe", lambda e: e.matmul(psD[:], lhsT=CT[:, g, tt * 128:(tt + 1) * 128], rhs=Hb[:, gs], start=True, stop=True),
                             reads=[f"CT{g}", f"Hb{g}"], writes=["psD"])
                        P.op("dve", lambda e: e.tensor_tensor(out=y3(yb[:]), in0=y3(psD[:]), in1=fbc(ea[:, g * 8:(g + 1) * 8], 64), op=ALU.mult), reads=["psD", smk + "ea"], writes=["yb"])
                        P.op("dve", lambda e: e.tensor_tensor(out=y3(Htmp[:]), in0=y3(Xtok[:, tt, gs]), in1=fbc(dsk[:, g * 8:(g + 1) * 8], 64), op=ALU.mult), reads=["Xtok", "dsk"], writes=["Htmp"])
                        yield
                        P.op("pe", lambda e: e.matmul(psD[:], lhsT=Btok[:, tt, g * 128:(g + 1) * 128], rhs=Xd[:, gs], start=True, stop=True),
                             reads=["Btok", "Xd"], writes=["psD"])
                        P.op("dve", lambda e: e.tensor_tensor(out=yb[:], in0=yb[:], in1=psY[:], op=ALU.add), reads=["yb", "psY"], writes=["yb"])
                        P.op("dve", lambda e: e.tensor_tensor(out=y3(Hs[:, gs]), in0=y3(Hs[:, gs]), in1=fbc(cd[:, g * 8:(g + 1) * 8], 64), op=ALU.mult), reads=[f"Hs{g}", smk + "cd"], writes=[f"Hs{g}"])
                        P.op("dve", lambda e: e.tensor_tensor(out=Hs[:, gs], in0=Hs[:, gs], in1=psD[:], op=ALU.add), reads=[f"Hs{g}", "psD"], writes=[f"Hs{g}"])
                        P.op("act", lambda e: e.activation(out=Hb[:, gs], in_=Hs[:, gs], func=AF.Copy), reads=[f"Hs{g}"], writes=[f"Hb{g}"])
                        yield
                        P.op("dve", lambda e: e.tensor_tensor(out=yb[:], in0=yb[:], in1=Htmp[:], op=ALU.add), reads=["yb", "Htmp"], writes=["yb"])
                        P.op("dve", lambda e: e.tensor_tensor(out=yb[:], in0=yb[:], in1=zs[:, tt, gs], op=ALU.mult), reads=["yb", "zs"], writes=["yb"])
                        rms_rstd(yb[:], "yb", 512, 1e-5, yo[:], small[:, 2:3], small[:, 3:4], "n2", jkey="yo")
                        yield
                        P.op("dve", lambda e: e.tensor_scalar(out=yo[:], in0=yb[:], scalar1=small[:, 3:4], scalar2=None, op0=ALU.mult), reads=["yb", "n2rs"], writes=["yo"])
                        yield
                        strot["m"] += 1
                        mb = strot["m"] % 2

                        def dst(g0, n, pt, pk):
                            P.op("dve", lambda e: e.tensor_tensor(out=mst[mb][:], in0=pt[:, 0:512].rearrange("p (a b) -> p a b", a=4), in1=fbc(snwT[:, g * 4:(g + 1) * 4], 128), op=ALU.mult),
                                 reads=[pk, "snwT"], writes=[f"mst{mb}"])
                        transposes_to(yo, "yo", 4, dst, None)
                        r0 = 2048 + g * 512
                        dstap = mixT_s.ap()[r0:r0 + 512, ot * 128:(ot + 1) * 128].rearrange("(a p) t -> p a t", p=128)
                        P.op("sp", lambda e: e.dma_start(out=dstap, in_=mst[mb][:]), reads=[f"mst{mb}"], pwrites=["mixT_s"], dsem=f"mst{mb}")
                        yield

                    units = [(tt, g) for tt in range(4) for g in range(4)]
                    chunk_ops(0)
                    yield
                    xd_op(0)
                    for _ in head(0, 0):
                        yield
                    for ui, (tt, g) in enumerate(units):
                        nxt = units[ui + 1] if ui + 1 < len(units) else None
                        if nxt is not None and nxt[1] == 0:
                            chunk_ops(nxt[0])
                        tl = tail(tt, g)
                        hd = head(*nxt) if nxt is not None else None
                        done_t = done_h = False
                        while not (done_t and (done_h or hd is None)):
                            if not done_t:
                                try:
                                    next(tl)
                                except StopIteration:
                                    done_t = True
                            if hd is not None and not done_h:
                                try:
                                    next(hd)
                                except StopIteration:
                                    done_h = True
                            yield
                        if nxt is not None and nxt[1] == 0:
                            xd_op(nxt[0])

                def th_ssd_w(blk, pend):
                    for _ in range(3):
                        yield
                    while pend:
                        pend.pop(0)()
                        yield
                    yield from (th_ssd_own(blk) if blk >= 6 else th_ssd(blk))

                interleave([th_norm(0)])
                for blk in range(8):
                    pend = xbc_dt_z(blk)
                    interleave([th_kvq(blk), th_ssd_w(blk, pend), th_norm(blk + 1) if blk < 7 else None])
                P.final_wait("sp")
                P.emit()

            with contextlib.ExitStack() as a2:
                sb = lambda name, shape, dt: a2.enter_context(nc.sbuf_tensor(name, shape, dt))
                kT = [sb(f"kT{i}", [128, 2, SEQ], BF16) for i in range(2)]
                vv = [sb(f"vv{i}", [128, 32, 257], BF16) for i in range(2)]
                qT = [sb(f"qT{i}", [128, 2, 1024], BF16) for i in range(2)]
                ab = [sb(f"ab{i}", [128, 5, 512], F32) for i in range(2)]
                kbias = sb("kbias_sb", [128, 512], F32)
                lam = sb("lam_sb", [128, 4, 128], F32)
                lsm = sb("lsm", [128, 8], F32)
                subw = sb("subw", [128, 256], F32)
                NS, NP = 3, 4
                Sb = [sb(f"Sb{i}", [128, 512], F32) for i in range(NS)]
                Pt = [sb(f"Pt{i}", [128, 512], BF16) for i in range(NP)]
                Oraw = sb("Oraw", [128, 8, 257], F32)
                dsm = sb("dsm", [128, 16], F32)
                dtile = sb("dtile", [128, 1024], F32)
                ajunk = sb("ajunk", [128, 256], BF16)
                atb = sb("atb", [128, 1024], BF16)
                ast = [sb(f"ast{i}", [128, 2, 512], BF16) for i in range(2)]
                ld = lambda eng, out, in_, key, dsem: P.op(eng, lambda e: e.dma_start(out=out, in_=in_), writes=[key], dsem=dsem)
                ld("sp", kbias[:], kbias_d.ap(), "kbias", "c1")
                ld("sp", subw[:], pbc(subw_d, 256), "subw", "c2")
                for i in range(4):
                    ld("sp", lam[:, i, :], pbc(lam_d, 128, off=i * 128), f"lam{i}", f"c{3 + i}")
                P.op("dve", lambda e: e.tensor_scalar(out=subw[:], in0=subw[:], scalar1=0.8, scalar2=None, op0=ALU.mult), reads=["subw"], writes=["subw"])
                for i in range(2):
                    P.op("dve", lambda e, i=i: e.tensor_tensor(out=lam[:, 2 * i, :], in0=lam[:, 2 * i, :], in1=lam[:, 2 * i + 1, :], op=ALU.mult), reads=[f"lam{2 * i}", f"lam{2 * i + 1}"], writes=[f"lam{2 * i}"])
                    P.op("act", lambda e, i=i: e.activation(out=lam[:, 2 * i + 1, :], in_=lam[:, 2 * i, :], func=AF.Copy, accum_out=lsm[:, i:i + 1]), reads=[f"lam{2 * i}"], writes=[f"lam{2 * i + 1}", f"lsm{i}"])
                    P.op("act", lambda e, i=i: e.activation(out=lsm[:, 2 + i:3 + i], in_=lsm[:, i:i + 1], func=AF.Exp), reads=[f"lsm{i}"], writes=[f"lse{i}"])
                P.op("dve", lambda e: e.tensor_tensor(out=lsm[:, 4:5], in0=lsm[:, 3:4], in1=lsm[:, 2:3], op=ALU.subtract), reads=["lse0", "lse1"], writes=["neglam"])
                P.op("dve", lambda e: e.tensor_scalar(out=lsm[:, 4:5], in0=lsm[:, 4:5], scalar1=-0.2, scalar2=None, op0=ALU.add), reads=["neglam"], writes=["neglam"])
                for i in range(2):
                    P.op("dve", lambda e, i=i: e.memset(vv[i][:, :, 256:257], 1.0), writes=[f"vv{i}"])
                sbanks = [(psA, "psA"), (psB, "psB"), (psY, "psY")]
                Ob = [(psC, "psC"), (psD, "psD"), (psM, "psM"), (psT1[:].bitcast(F32), "psT1")]
                trs[:] = [(psT0, "psT0")]

                def load_head(h):
                    hb_ = h % 2
                    P.op("sp", lambda e: e.dma_start(out=kT[hb_][:], in_=kT_s.ap()[h * 256:(h + 1) * 256, :].rearrange("(c p) t -> p c t", p=128)),
                         reads=["kT_s"], writes=[f"kT{hb_}"], dsem=f"kT{hb_}")
                    P.op("act", lambda e: e.dma_start(out=vv[hb_][:, :, 0:256], in_=v_s.ap()[:, h * 256:(h + 1) * 256].rearrange("(t p) e -> p t e", p=128)),
                         reads=["v_s"], writes=[f"vv{hb_}"], dsem=f"vv{hb_}")
                    P.op("sp", lambda e: e.dma_start(out=qT[hb_][:], in_=qT_s.ap()[h * 256:(h + 1) * 256, :].rearrange("(c p) t -> p c t", p=128)),
                         reads=["qT_s"], writes=[f"qT{hb_}"], dsem=f"qT{hb_}")
                    P.op("sp", lambda e: e.dma_start(out=ab[hb_][:], in_=abias_d.ap()[h].rearrange("p (a b) -> p a b", a=5)),
                         writes=[f"ab{hb_}"], dsem=f"ab{hb_}")

                tiles = [(h, qb, c, kb) for h in range(NH) for qb in range(2) for c in range(2) for kb in range(28 + 4 * qb)]
                arot = {"a": 0}

                def front(i):
                    h, qb, c, kb = tiles[i]
                    hb_ = h % 2
                    jd = kb - (24 + 4 * qb)
                    acc, ak = sbanks[i % 3]
                    P.op("pe", lambda e: e.matmul(acc[:], lhsT=kT[hb_][:, c, kb * 128:(kb + 1) * 128], rhs=qT[hb_][:, c, qb * 512:(qb + 1) * 512], start=True, stop=True),
                         reads=[f"kT{hb_}", f"qT{hb_}"], writes=[ak])
                    si = i % NS
                    bt = ab[hb_][:, (jd + 1) if jd >= 0 else 0, :]
                    P.op("dve", lambda e: e.tensor_tensor(out=Sb[si][:], in0=acc[:], in1=bt, op=ALU.add), reads=[ak, f"ab{hb_}"], writes=[f"Sb{si}"])
                    pi = i % NP
                    col = h * 64 + qb * 32 + kb
                    P.op("act", lambda e: e.activation(out=Pt[pi][:], in_=Sb[si][:], func=AF.Exp, bias=kbias[:, col:col + 1], scale=SCALE),
                         reads=[f"Sb{si}", "kbias"], writes=[f"Pt{pi}"])

                def back(i):
                    h, qb, c, kb = tiles[i]
                    hb_ = h % 2
                    nkb = 28 + 4 * qb
                    pi = i % NP

                    def fn(e):
                        ins = None
                        for s in range(4):
                            ins = e.matmul(Ob[s][0][:, 0:257], lhsT=Pt[pi][:, s * 128:(s + 1) * 128], rhs=vv[hb_][:, kb, 0:257], start=(kb == 0), stop=(kb == nkb - 1))
                        return ins
                    P.op("pe", fn, reads=[f"Pt{pi}", f"vv{hb_}"], writes=[o[1] for o in Ob])
                    if kb != nkb - 1:
                        return
                    for s in range(4):
                        if s < 2:
                            P.op("act", lambda e, s=s: e.activation(out=Oraw[:, c * 4 + s, :], in_=Ob[s][0][:, 0:257], func=AF.Copy), reads=[Ob[s][1]], writes=[f"Oraw{c}_{s}"])
                        else:
                            P.op("dve", lambda e, s=s: e.tensor_copy(out=Oraw[:, c * 4 + s, :], in_=Ob[s][0][:, 0:257]), reads=[Ob[s][1]], writes=[f"Oraw{c}_{s}"])
                    if c != 1:
                        return

                    ork = [f"Oraw{cc}_{s}" for cc in range(2) for s in range(4)]

                    def e_a():
                        P.op("dve", lambda e: e.reciprocal(out=dsm[:, 0:8], in_=Oraw[:, :, 256]), reads=ork, writes=["dsmr"])
                        P.op("dve", lambda e: e.tensor_scalar(out=dsm[:, 4:8], in0=dsm[:, 4:8], scalar1=lsm[:, 4:5], scalar2=None, op0=ALU.mult), reads=["dsmr", "neglam"], writes=["dsmr"])

                    def e_b(s):
                        def f():
                            P.op("dve", lambda e: e.tensor_scalar(out=dtile[:, s * 256:(s + 1) * 256], in0=Oraw[:, 4 + s, 0:256], scalar1=dsm[:, 4 + s:5 + s], scalar2=None, op0=ALU.mult),
                                 reads=ork + ["dsmr"], writes=[f"dtile{s}"])
                            P.op("dve", lambda e: e.scalar_tensor_tensor(out=dtile[:, s * 256:(s + 1) * 256], in0=Oraw[:, s, 0:256], scalar=dsm[:, s:s + 1], in1=dtile[:, s * 256:(s + 1) * 256], op0=ALU.mult, op1=ALU.add),
                                 reads=ork + ["dsmr", f"dtile{s}"], writes=[f"dtile{s}"])
                        return f

                    def e_c(s):
                        def f():
                            P.op("act", lambda e: e.activation(out=ajunk[:], in_=dtile[:, s * 256:(s + 1) * 256], func=AF.Square, accum_out=dsm[:, 8 + s:9 + s]), reads=[f"dtile{s}"], writes=["ajunk", f"ass{s}"])
                        return f

                    def e_d():
                        P.op("dve", lambda e: e.tensor_scalar(out=dsm[:, 12:16], in0=dsm[:, 8:12], scalar1=1.0 / 256, scalar2=1e-5, op0=ALU.mult, op1=ALU.add), reads=[f"ass{s}" for s in range(4)], writes=["ars"])

                    def e_e():
                        P.op("act", lambda e: e.activation(out=dsm[:, 12:16], in_=dsm[:, 12:16], func=AF.Sqrt), reads=["ars"], writes=["ars"])

                    def e_f():
                        P.op("dve", lambda e: e.reciprocal(out=dsm[:, 12:16], in_=dsm[:, 12:16]), reads=["ars"], writes=["ars"])

                    def e_g(s):
                        def f():
                            P.op("dve", lambda e: e.scalar_tensor_tensor(out=atb[:, s * 256:(s + 1) * 256], in0=dtile[:, s * 256:(s + 1) * 256], scalar=dsm[:, 12 + s:13 + s], in1=subw[:], op0=ALU.mult, op1=ALU.mult),
                                 reads=[f"dtile{s}", "ars", "subw"], writes=[f"atb{s}"])
                        return f

                    def e_h():
                        arot["a"] += 1
                        ai = arot["a"] % 2
                        pt, pk = next_tr()

                        def fn(e):
                            ins = None
                            for j in range(8):
                                ins = e.transpose(out=pt[:, j * 128:(j + 1) * 128], in_=atb[:, j * 128:(j + 1) * 128], identity=identb[:])
                            return ins
                        P.op("pe", fn, reads=[f"atb{s}" for s in range(4)] + ["identb"], writes=[pk])

                        def e_i():
                            P.op("act", lambda e: e.activation(out=ast[ai][:].rearrange("p e (s t) -> p s e t", s=4), in_=pt[:, 0:1024].rearrange("p (s e t) -> p s e t", s=4, e=2), func=AF.Copy),
                                 reads=[pk], writes=[f"ast{ai}"])
                            dstap = mixT_s.ap()[h * 256:(h + 1) * 256, qb * 512:(qb + 1) * 512].rearrange("(e p) t -> p e t", p=128)
                            P.op("sp", lambda e: e.dma_start(out=dstap, in_=ast[ai][:]), reads=[f"ast{ai}"], pwrites=["mixT_s"], dsem=f"ast{ai}")
                        deferred.append((i + 24, e_i))
                    steps = [(2, e_a), (3, e_b(0)), (4, e_b(1)), (5, e_b(2)), (6, e_b(3)), (8, e_c(0)), (9, e_c(1)), (10, e_c(2)), (11, e_c(3)),
                             (13, e_d), (15, e_e), (17, e_f), (18, e_g(0)), (19, e_g(1)), (20, e_g(2)), (21, e_g(3)), (23, e_h)]
                    for off, f in steps:
                        deferred.append((i + off, f))

                deferred = []
                SK = 2
                load_head(0)
                load_head(1)
                nt = len(tiles)
                for i in range(nt + SK):
                    if i < nt:
                        front(i)
                    if i >= SK:
                        back(i - SK)
                        while deferred and deferred[0][0] <= i - SK:
                            deferred.pop(0)[1]()
                        h_done = tiles[i - SK][0]
                        if (i - SK == nt - 1 or tiles[i - SK + 1][0] != h_done) and h_done + 2 < NH:
                            load_head(h_done + 2)
                while deferred:
                    deferred.pop(0)[1]()
                P.final_wait("sp")
                P.emit()

            trs[:] = [(psT0, "psT0"), (psT1, "psT1")]
            with contextlib.ExitStack() as b1:
                sb = lambda name, shape, dt: b1.enter_context(nc.sbuf_tensor(name, shape, dt))
                mixT = sb("mixT", [128, 32, 1024], BF16)
                x1 = sb("x1", [128, 8, D], F32)
                wo = [sb(f"wo{i}", [128, 32, 128], BF16) for i in range(2)]
                nffn = sb("nffn", [128, D], F32)
                junk2 = sb("junk2", [128, D], BF16)
                hb2 = sb("hb2", [128, D], BF16)
                sm3 = sb("sm3", [128, 4], F32)
                h2st = [sb(f"h2st{i}", [128, 16, 128], BF16) for i in range(2)]
                P.op("sp", lambda e: e.dma_start(out=nffn[:], in_=pbc(nffn_d, D)), writes=["nffn"], dsem="c1")
                for i in range(4):
                    P.op("sp", lambda e, i=i: e.dma_start(out=mixT[:, i * 8:(i + 1) * 8, :], in_=mixT_s.ap()[i * 1024:(i + 1) * 1024, :].rearrange("(c p) t -> p c t", p=128)),
                         reads=["mixT_s"], pwrites=["mixT"], dsem="mixT")
                for tt in range(8):
                    P.op("act", lambda e, tt=tt: e.dma_start(out=x1[:, tt, :], in_=xctx.ap()[3072 + tt * 128:3072 + (tt + 1) * 128, :]), pwrites=["x1all"], dsem="x1ld")
                for nb in range(16):
                    b = nb % 2
                    src = w_out.ap().rearrange("(kc p) n -> p kc n", p=128)[:, :, nb * 128:(nb + 1) * 128]
                    P.op("pool", lambda e, b=b, src=src: e.dma_start(out=wo[b][:], in_=src), writes=[f"wo{b}"], dsem=f"wo{b}")
                    for tt in range(8):
                        acc, ak = next_acc()

                        def fn(e, acc=acc, tt=tt, b=b):
                            ins = None
                            for kc in range(32):
                                ins = e.matmul(acc[:, 0:128], lhsT=mixT[:, kc, tt * 128:(tt + 1) * 128], rhs=wo[b][:, kc, :], start=(kc == 0), stop=(kc == 31))
                            return ins
                        P.op("pe", fn, reads=["mixT", f"wo{b}"], writes=[ak])
                        P.op("dve", lambda e, acc=acc, tt=tt, nb=nb: e.tensor_tensor(out=x1[:, tt, nb * 128:(nb + 1) * 128], in0=x1[:, tt, nb * 128:(nb + 1) * 128], in1=acc[:, 0:128], op=ALU.add),
                             reads=[ak, "x1all", f"x1_{tt}"], writes=[f"x1_{tt}"])
                for tt in range(8):
                    P.op("sp", lambda e, tt=tt: e.dma_start(out=x1_s.ap()[tt * 128:(tt + 1) * 128, :], in_=x1[:, tt, :]), reads=[f"x1_{tt}"], pwrites=["x1_s"], dsem="x1st")
                    rms_rstd(x1[:, tt, :], f"x1_{tt}", D, 1e-6, junk2[:], sm3[:, 0:1], sm3[:, 1:2], "n3")
                    P.op("dve", lambda e, tt=tt: e.scalar_tensor_tensor(out=hb2[:], in0=x1[:, tt, :], scalar=sm3[:, 1:2], in1=nffn[:], op0=ALU.mult, op1=ALU.mult),
                         reads=[f"x1_{tt}", "n3rs", "nffn"], writes=["hb2"])
                    hi = tt % 2

                    def dst(g0, n, pt, pk, hi=hi):
                        P.op("act", lambda e: e.activation(out=h2st[hi][:, g0:g0 + n, :], in_=pt[:, 0:n * 128].rearrange("p (a b) -> p a b", a=n), func=AF.Copy),
                             reads=[pk], writes=[f"h2st{hi}_{g0}"])
                    transposes_to(hb2, "hb2", 16, dst, None)
                    dstap = h2T_s.ap()[:, tt * 128:(tt + 1) * 128].rearrange("(c p) t -> p c t", p=128)
                    P.op("sp", lambda e, hi=hi, dstap=dstap: e.dma_start(out=dstap, in_=h2st[hi][:]), reads=[f"h2st{hi}_0", f"h2st{hi}_8"], pwrites=["h2T_s"], dsem=f"h2st{hi}")
                    P.reads_since.setdefault(f"h2st{hi}_0", []).append(("d_" + f"h2st{hi}", P.count["d_" + f"h2st{hi}"]))
                    P.reads_since.setdefault(f"h2st{hi}_8", []).append(("d_" + f"h2st{hi}", P.count["d_" + f"h2st{hi}"]))
                P.final_wait("sp")
                P.emit()

            with contextlib.ExitStack() as b0:
                sbp = lambda name, shape, dt: b0.enter_context(nc.sbuf_tensor(name, shape, dt))
                aT = sbp("aT", [128, 44, 1024], BF16)
                wd = [sbp(f"wd{i}", [128, 44, 128], BF16) for i in range(1)]
                with contextlib.ExitStack() as b2:
                    sb = lambda name, shape, dt: b2.enter_context(nc.sbuf_tensor(name, shape, dt))
                    h2T = sb("h2T", [128, 16, 1024], BF16)
                    wg = [sb(f"wg{i}", [128, 16, 512], BF16) for i in range(2)]
                    wu = [sb(f"wu{i}", [128, 16, 512], BF16) for i in range(2)]
                    sg = [sb(f"sg{i}", [128, 512], F32) for i in range(2)]
                    for i in range(2):
                        P.op("sp", lambda e, i=i: e.dma_start(out=h2T[:, i * 8:(i + 1) * 8, :], in_=h2T_s.ap()[i * 1024:(i + 1) * 1024, :].rearrange("(c p) t -> p c t", p=128)),
                             reads=["h2T_s"], pwrites=["h2T"], dsem="h2T")
                    grot = 0
                    for fg in range(11):
                        b = fg % 2
                        srcg = w_gate.ap().rearrange("(kc p) n -> p kc n", p=128)[:, :, fg * 512:(fg + 1) * 512]
                        srcu = w_up.ap().rearrange("(kc p) n -> p kc n", p=128)[:, :, fg * 512:(fg + 1) * 512]
                        if fg == 0:
                            for j4 in range(4):
                                P.op("pool", lambda e, b=b, srcg=srcg, j4=j4: e.dma_start(out=wg[b][:, :, j4 * 128:(j4 + 1) * 128], in_=srcg[:, :, j4 * 128:(j4 + 1) * 128]), writes=[f"wg{b}_{j4}"], dsem=f"wg{b}_{j4}")
                                P.op("pool", lambda e, b=b, srcu=srcu, j4=j4: e.dma_start(out=wu[b][:, :, j4 * 128:(j4 + 1) * 128], in_=srcu[:, :, j4 * 128:(j4 + 1) * 128]), writes=[f"wu{b}_{j4}"], dsem=f"wu{b}_{j4}")
                        else:
                            P.op("pool", lambda e, b=b, srcg=srcg: e.dma_start(out=wg[b][:], in_=srcg), writes=[f"wg{b}"], dsem=f"wg{b}")
                            P.op("pool", lambda e, b=b, srcu=srcu: e.dma_start(out=wu[b][:], in_=srcu), writes=[f"wu{b}"], dsem=f"wu{b}")
                        for j in range(4):
                            fc = fg * 4 + j
                            for half in range(2):
                                hs = slice(half * 512, (half + 1) * 512)
                                gacc, gk = (psC, "psC") if half == 0 else (psD, "psD")
                                uacc, uk = (psM, "psM") if half == 0 else (psY, "psY")

                                def fn(e, gacc=gacc, b=b, j=j, hs=hs):
                                    ins = None
                                    for kc in range(16):
                                        ins = e.matmul(gacc[:], lhsT=wg[b][:, kc, j * 128:(j + 1) * 128], rhs=h2T[:, kc, hs], start=(kc == 0), stop=(kc == 15))
                                    return ins
                                P.op("pe", fn, reads=[f"wg{b}", "h2T"] + ([f"wg{b}_{j}"] if fg == 0 else []), writes=[gk])

                                def fn(e, uacc=uacc, b=b, j=j, hs=hs):
                                    ins = None
                                    for kc in range(16):
                                        ins = e.matmul(uacc[:], lhsT=wu[b][:, kc, j * 128:(j + 1) * 128], rhs=h2T[:, kc, hs], start=(kc == 0), stop=(kc == 15))
                                    return ins
                                P.op("pe", fn, reads=[f"wu{b}", "h2T"] + ([f"wu{b}_{j}"] if fg == 0 else []), writes=[uk])
                                grot += 1
                                gi = grot % 2
                                P.op("act", lambda e, gacc=gacc, gi=gi: e.activation(out=sg[gi][:], in_=gacc[:], func=AF.Silu), reads=[gk], writes=[f"sg{gi}"])
                                P.op("dve", lambda e, uacc=uacc, gi=gi, fc=fc, hs=hs: e.tensor_tensor(out=aT[:, fc, hs], in0=sg[gi][:], in1=uacc[:], op=ALU.mult), reads=[f"sg{gi}", uk], writes=["aT"])
                    for nb in range(1):
                        src = w_down.ap().rearrange("(kc p) n -> p kc n", p=128)[:, :, nb * 128:(nb + 1) * 128]
                        P.op("pool", lambda e, nb=nb, src=src: e.dma_start(out=wd[nb][:], in_=src), writes=[f"wd{nb}"], dsem=f"wd{nb}")
                    P.final_wait("sp")
                P.emit()
                with contextlib.ExitStack() as b3:
                    sb = lambda name, shape, dt: b3.enter_context(nc.sbuf_tensor(name, shape, dt))
                    wd.append(sb("wd1", [128, 44, 128], BF16))
                    wd.append(sb("wd2", [128, 44, 128], BF16))
                    ys = sb("ys", [128, 8, D], F32)
                    nfin = sb("nfin", [128, D], F32)
                    junk3 = sb("junk3", [128, D], BF16)
                    sm4 = sb("sm4", [128, 4], F32)
                    P.op("sp", lambda e: e.dma_start(out=nfin[:], in_=pbc(nfin_d, D)), writes=["nfin"], dsem="c1")
                    for tt in range(8):
                        P.op("sp", lambda e, tt=tt: e.dma_start(out=ys[:, tt, :], in_=x1_s.ap()[tt * 128:(tt + 1) * 128, :]), reads=["x1_s"], pwrites=["ysall"], dsem="ysld")
                    for nb in range(16):
                        b = nb % 3
                        src = w_down.ap().rearrange("(kc p) n -> p kc n", p=128)[:, :, nb * 128:(nb + 1) * 128]
                        if nb >= 1:
                            P.op("pool", lambda e, b=b, src=src: e.dma_start(out=wd[b][:], in_=src), writes=[f"wd{b}"], dsem=f"wd{b}")
                        for tt in range(8):
                            acc, ak = next_acc()

                            def fn(e, acc=acc, tt=tt, b=b):
                                ins = None
                                for kc in range(44):
                                    ins = e.matmul(acc[:, 0:128], lhsT=aT[:, kc, tt * 128:(tt + 1) * 128], rhs=wd[b][:, kc, :], start=(kc == 0), stop=(kc == 43))
                                return ins
                            P.op("pe", fn, reads=["aT", f"wd{b}"], writes=[ak])
                            P.op("dve", lambda e, acc=acc, tt=tt, nb=nb: e.tensor_tensor(out=ys[:, tt, nb * 128:(nb + 1) * 128], in0=ys[:, tt, nb * 128:(nb + 1) * 128], in1=acc[:, 0:128], op=ALU.add),
                                 reads=[ak, "ysall", f"ys{tt}"], writes=[f"ys{tt}"])
                    for tt in range(8):
                        rms_rstd(ys[:, tt, :], f"ys{tt}", D, 1e-6, junk3[:], sm4[:, 0:1], sm4[:, 1:2], "n4")
                        P.op("dve", lambda e, tt=tt: e.scalar_tensor_tensor(out=ys[:, tt, :], in0=ys[:, tt, :], scalar=sm4[:, 1:2], in1=nfin[:], op0=ALU.mult, op1=ALU.mult),
                             reads=[f"ys{tt}", "n4rs", "nfin"], writes=[f"ys{tt}"])
                        P.op("sp", lambda e, tt=tt: e.dma_start(out=out_d.ap()[tt * 128:(tt + 1) * 128, :], in_=ys[:, tt, :]), reads=[f"ys{tt}"], pwrites=["out"], dsem="outst")
                    P.final_wait("sp")
                P.emit()
    return nc


def _consts():
    k = np.arange(128)
    ident = np.eye(128, dtype=np.float32)
    tri = (k[:, None] <= k[None, :]).astype(np.float32)
    U = (k[:, None] > k[None, :]).astype(np.float32)
    ones = np.ones((128, 128), np.float32)
    cst = np.concatenate([ident, tri, U, ones], axis=1)
    start = 2.0 ** (-8.0 / NH)
    slopes = np.array([start ** (i + 1) for i in range(NH)], dtype=np.float64)
    kl = np.arange(128)[:, None].astype(np.float64)
    ql = np.arange(512)[None, :].astype(np.float64)
    ab = np.zeros((NH, 128, 5, 512), np.float32)
    for h in range(NH):
        gen = slopes[h] * (kl - ql) / SCALE
        ab[h, :, 0, :] = gen
        for jd in range(4):
            masked = ql < (128 * jd + kl)
            ab[h, :, 1 + jd, :] = np.where(masked, NEG / SCALE, gen)
    return cst, ab.reshape(NH, 128, 5 * 512), slopes


_CACHE = {}


def kernel(x, norm_mix_w, w_in, lambda_q1, lambda_k1, lambda_q2, lambda_k2, subln_w,
           conv_w, conv_b, dt_bias, a_log, d_skip, ssm_norm_w, w_out,
           norm_ffn_w, w_gate, w_up, w_down, norm_final_w):
    f = lambda a: np.ascontiguousarray(np.asarray(a, dtype=np.float32))
    x = f(x)
    if "nc" not in _CACHE:
        _CACHE["nc"] = build_program()
    nc = _CACHE["nc"]
    cst, abias, slopes = _consts()
    cw = f(conv_w)[0]
    cwT = np.ascontiguousarray(cw.reshape(4, 24, 128).transpose(2, 1, 0)).reshape(128, 96)
    cbT = np.ascontiguousarray(f(conv_b)[0].reshape(24, 128).T)
    lam4 = np.stack([f(lambda_q1)[0], f(lambda_k1)[0], f(lambda_q2)[0], f(lambda_k2)[0]], axis=0)
    shared = {
        "w_in": f(w_in)[0], "w_out": f(w_out)[0], "w_gate": f(w_gate)[0], "w_up": f(w_up)[0], "w_down": f(w_down)[0],
        "nmixT": np.ascontiguousarray(f(norm_mix_w).reshape(16, 128).T), "norm_ffn_w": f(norm_ffn_w).reshape(1, D), "norm_final_w": f(norm_final_w).reshape(1, D),
        "lam4": np.ascontiguousarray(lam4), "subln_w": f(subln_w).reshape(1, 256), "cwT": cwT, "cbT": cbT,
        "dt_bias": f(dt_bias).reshape(1, 32), "a_log": f(a_log).reshape(1, 32), "d_skip": f(d_skip).reshape(1, 32),
        "snwT": np.ascontiguousarray(f(ssm_norm_w).reshape(16, 128).T), "cst": cst, "abias": abias,
    }
    in_maps = []
    for c in range(8):
        b, j = c // 4, c % 4
        pad = (3 - j) * 1024
        xc = np.zeros((SEQ, D), np.float32)
        xc[pad:] = x[b, 0:(j + 1) * 1024]
        tok = np.arange(SEQ).reshape(32, 128).T
        vmask = (tok >= pad).astype(np.float32)
        kb = np.zeros((128, 512), np.float32)
        for h in range(NH):
            for qb in range(2):
                for kbi in range(32):
                    v = slopes[h] * (kbi * 128 - (3072 + qb * 512)) + (NEG if kbi * 128 < pad else 0.0)
                    kb[:, h * 64 + qb * 32 + kbi] = v
        m = dict(shared)
        m.update({"xctx": xc, "vmask": np.ascontiguousarray(vmask), "kbias": kb})
        in_maps.append(m)
    res = run_bass_kernel_spmd(nc, in_maps, core_ids=list(range(8)))
    out = np.zeros((2, SEQ, D), np.float32)
    for c in range(8):
        b, j = c // 4, c % 4
        out[b, j * 1024:(j + 1) * 1024] = res.results[c]["out"]
    return out
```
